# Optimizing a Trainium2 kernel written in Bass

```python
import jax, jax.numpy as jnp
from jax import lax
import numpy as np

D_MODEL = 1024
BATCH = 8
SEQ = 8192
DEPTH = 2
DEC_BATCH = 4
DEC_SEQ = 4096
PAST_LEN = 128

N_HEADS = 8
Q_LORA = 256
KV_LORA = 128
QK_NOPE = 64
QK_ROPE = 32
V_HEAD = 64
ROPE_BASE = 10000.0
ATTN_SCALE = (QK_NOPE + QK_ROPE) ** -0.5
Q_BLOCK = 128
CONV_CH = 512
CONV_W = 31
CONV_PAD = (CONV_W - 1) // 2
D_FF = 4 * D_MODEL
PLE_DIM = 256
EPS = 1e-6

OFF_KV = Q_LORA
OFF_KR = OFF_KV + KV_LORA
OFF_CONV = OFF_KR + QK_ROPE
OFF_GATE = OFF_CONV + 2 * CONV_CH
IN_COLS = OFF_GATE + 2 * D_MODEL

kernel_name = "mla_conformer_gated_hybrid_encoder"


def _rms(x, g):
    xf = x.astype(jnp.float32)
    y = xf * lax.rsqrt(jnp.mean(xf * xf, axis=-1, keepdims=True) + EPS)
    return (y * g.astype(jnp.float32)).astype(x.dtype)


def _rope_tables(seq, dtype):
    inv = 1.0 / (ROPE_BASE ** (jnp.arange(0, QK_ROPE, 2, dtype=jnp.float32) / QK_ROPE))
    ang = jnp.arange(seq, dtype=jnp.float32)[:, None] * inv[None, :]
    return jnp.cos(ang).astype(dtype), jnp.sin(ang).astype(dtype)


def _rope(x, cos, sin):
    half = QK_ROPE // 2
    x1, x2 = x[..., :half], x[..., half:]
    return jnp.concatenate([x1 * cos - x2 * sin, x1 * sin + x2 * cos], axis=-1)


def _mla_attention(q_nope, q_rope, k_nope, k_rope, v):
    b, s, h, _ = q_nope.shape
    nb = s // Q_BLOCK

    def block(args):
        qn, qr = args
        sc = (jnp.einsum('bqhn,bkhn->bhqk', qn, k_nope, preferred_element_type=jnp.float32)
              + jnp.einsum('bqhr,bkr->bhqk', qr, k_rope, preferred_element_type=jnp.float32)) * ATTN_SCALE
        pr = jax.nn.softmax(sc, axis=-1).astype(v.dtype)
        return jnp.einsum('bhqk,bkhv->bqhv', pr, v)

    qn_b = q_nope.reshape(b, nb, Q_BLOCK, h, QK_NOPE).transpose(1, 0, 2, 3, 4)
    qr_b = q_rope.reshape(b, nb, Q_BLOCK, h, QK_ROPE).transpose(1, 0, 2, 3, 4)
    out = lax.map(block, (qn_b, qr_b))
    return out.transpose(1, 0, 2, 3, 4).reshape(b, s, h * V_HEAD)


def _conv_branch(zc, conv_w, conv_b, ln_g, ln_b, w_pc):
    a, gt = zc[..., :CONV_CH], zc[..., CONV_CH:]
    glu = a * jax.nn.sigmoid(gt)
    y = lax.conv_general_dilated(glu, conv_w[:, None, :], window_strides=(1,),
                                 padding=[(CONV_PAD, CONV_PAD)],
                                 dimension_numbers=('NWC', 'WIO', 'NWC'),
                                 feature_group_count=CONV_CH) + conv_b
    yf = y.astype(jnp.float32)
    mu = jnp.mean(yf, axis=-1, keepdims=True)
    var = jnp.mean(jnp.square(yf - mu), axis=-1, keepdims=True)
    yn = ((yf - mu) * lax.rsqrt(var + EPS) * ln_g.astype(jnp.float32) + ln_b.astype(jnp.float32)).astype(zc.dtype)
    return jax.nn.silu(yn) @ w_pc


def _layer(x, p, cos, sin, g_mix, w_in, g_q, w_uq, g_kv, w_ukv, w_oa, conv_w, conv_b, ln_g, ln_b, w_pc,
           b_gate, w_out, g_mlp, w_up, w_down, w_ple_gate, w_ple, g_ple):
    b, s, _ = x.shape
    u = _rms(x, g_mix)
    z = u @ w_in
    cq = z[..., :OFF_KV]
    ckv = z[..., OFF_KV:OFF_KR]
    kr = z[..., OFF_KR:OFF_CONV]
    zc = z[..., OFF_CONV:OFF_GATE]
    zg = z[..., OFF_GATE:]
    q = (_rms(cq, g_q) @ w_uq).reshape(b, s, N_HEADS, QK_NOPE + QK_ROPE)
    q_nope = q[..., :QK_NOPE]
    q_rope = _rope(q[..., QK_NOPE:], cos[:, None, :], sin[:, None, :])
    k_rope = _rope(kr, cos, sin)
    kv = (_rms(ckv, g_kv) @ w_ukv).reshape(b, s, N_HEADS, QK_NOPE + V_HEAD)
    k_nope, v = kv[..., :QK_NOPE], kv[..., QK_NOPE:]
    attn = _mla_attention(q_nope, q_rope, k_nope, k_rope, v) @ w_oa
    conv = _conv_branch(zc, conv_w, conv_b, ln_g, ln_b, w_pc)
    gates = jax.nn.sigmoid(zg + b_gate)
    merged = gates[..., :D_MODEL] * attn + gates[..., D_MODEL:] * conv
    x = x + merged @ w_out
    h = _rms(x, g_mlp) @ w_up
    x = x + jnp.square(jax.nn.relu(h)) @ w_down
    x = x + jax.nn.sigmoid(x @ w_ple_gate) * _rms(p @ w_ple, g_ple)
    return x


def setup_inputs(seed: int = 0) -> dict:
    key = jax.random.key(seed)
    ks = jax.random.split(key, 32)

    def nrm(k, shape, scale):
        return jax.random.normal(k, shape, jnp.float32) * scale

    def gain(k, shape):
        return 1.0 + 0.01 * jax.random.normal(k, shape, jnp.float32)

    L = DEPTH
    return {
        "x_prompt": nrm(ks[0], (BATCH, SEQ, D_MODEL), 1.0),
        "x_sample": nrm(ks[1], (DEC_BATCH, DEC_SEQ, D_MODEL), 1.0),
        "p_prompt": nrm(ks[2], (DEPTH, BATCH, SEQ, PLE_DIM), 1.0),
        "p_sample": nrm(ks[3], (DEPTH, DEC_BATCH, DEC_SEQ, PLE_DIM), 1.0),
        "g_mix": gain(ks[4], (L, D_MODEL)),
        "w_in": nrm(ks[5], (L, D_MODEL, IN_COLS), D_MODEL ** -0.5),
        "g_q": gain(ks[6], (L, Q_LORA)),
        "w_uq": nrm(ks[7], (L, Q_LORA, N_HEADS * (QK_NOPE + QK_ROPE)), Q_LORA ** -0.5),
        "g_kv": gain(ks[8], (L, KV_LORA)),
        "w_ukv": nrm(ks[9], (L, KV_LORA, N_HEADS * (QK_NOPE + V_HEAD)), KV_LORA ** -0.5),
        "w_oa": nrm(ks[10], (L, N_HEADS * V_HEAD, D_MODEL), (N_HEADS * V_HEAD) ** -0.5),
        "conv_w": nrm(ks[11], (L, CONV_W, CONV_CH), CONV_W ** -0.5),
        "conv_b": nrm(ks[12], (L, CONV_CH), 0.01),
        "ln_g": gain(ks[13], (L, CONV_CH)),
        "ln_b": nrm(ks[14], (L, CONV_CH), 0.01),
        "w_pc": nrm(ks[15], (L, CONV_CH, D_MODEL), CONV_CH ** -0.5),
        "b_gate": nrm(ks[16], (L, 2 * D_MODEL), 0.01),
        "w_out": nrm(ks[17], (L, D_MODEL, D_MODEL), D_MODEL ** -0.5),
        "g_mlp": gain(ks[18], (L, D_MODEL)),
        "w_up": nrm(ks[19], (L, D_MODEL, D_FF), D_MODEL ** -0.5),
        "w_down": nrm(ks[20], (L, D_FF, D_MODEL), D_FF ** -0.5),
        "w_ple_gate": nrm(ks[21], (L, D_MODEL, D_MODEL), D_MODEL ** -0.5),
        "w_ple": nrm(ks[22], (L, PLE_DIM, D_MODEL), PLE_DIM ** -0.5),
        "g_ple": gain(ks[23], (L, D_MODEL)),
        "g_final": gain(ks[24], (D_MODEL,)),
    }


def reference(x_prompt, x_sample, p_prompt, p_sample, g_mix, w_in, g_q, w_uq, g_kv, w_ukv, w_oa,
              conv_w, conv_b, ln_g, ln_b, w_pc, b_gate, w_out, g_mlp, w_up, w_down,
              w_ple_gate, w_ple, g_ple, g_final):
    def run(x, p):
        cos, sin = _rope_tables(x.shape[1], x.dtype)
        for i in range(DEPTH):
            x = _layer(x, p[i], cos, sin, g_mix[i], w_in[i], g_q[i], w_uq[i], g_kv[i], w_ukv[i], w_oa[i],
                       conv_w[i], conv_b[i], ln_g[i], ln_b[i], w_pc[i], b_gate[i], w_out[i],
                       g_mlp[i], w_up[i], w_down[i], w_ple_gate[i], w_ple[i], g_ple[i])
        return _rms(x, g_final)

    y_prompt = run(x_prompt, p_prompt)
    y_sample = run(x_sample, p_sample)
    return (y_prompt, y_sample)
```

```python
import numpy as np
from contextlib import ExitStack
import concourse.bass as bass
import concourse.mybir as mybir
from concourse.bass_utils import run_bass_kernel_spmd

F32 = mybir.dt.float32
BF16 = mybir.dt.bfloat16
AF = mybir.ActivationFunctionType
ALU = mybir.AluOpType

D = 1024
DEPTH = 2
NH = 8
QL = 256
KVL = 128
NOPE = 64
ROPE = 32
VH = 64
CCH = 512
CW = 31
PAD = 15
DFF = 4096
PLE = 256
EPS = 1e-6
ATTN_SCALE = float((NOPE + ROPE) ** -0.5)
T = 512
NCST = 128


class _Op:
    __slots__ = ("stream", "fn", "deps", "dma", "needs_inc", "incval", "dsem", "dval", "waits")

    def __init__(self, stream, fn, dma):
        self.stream = stream
        self.fn = fn
        self.deps = set()
        self.dma = dma
        self.needs_inc = False
        self.incval = None
        self.dsem = None
        self.dval = None
        self.waits = None


def _region(ap):
    t = ap.tensor
    shp = list(t.shape)
    fsz = 1
    for d in shp[1:]:
        fsz *= int(d)
    off = int(ap.offset)
    dims = [(int(s), int(c)) for s, c in ap.ap]
    p0 = off // fsz
    f0 = off % fsz
    if dims and dims[0][0] == fsz:
        pc = dims[0][1]
        rest = dims[1:]
    else:
        pc = 1
        rest = dims
    span = 1
    for s, c in rest:
        span += (c - 1) * abs(s)
    return (t.name, p0, p0 + pc, f0, f0 + span)


class Sched:
    STREAMS = ("pe", "act", "dve", "pool", "sp")
    KDMA = 8

    def __init__(self, nc, es):
        self.nc = nc
        self.ops = []
        self.track = {}
        self.sem = {s: es.enter_context(nc.semaphore("s_" + s)) for s in ("pe", "act", "dve", "pool")}
        self.dsem = {s: [es.enter_context(nc.semaphore("d_%s%d" % (s, j))) for j in range(self.KDMA)]
                     for s in ("sp", "pool")}
        self.cnt = {s: 0 for s in self.sem}
        self.dcnt = {s: [0] * self.KDMA for s in self.dsem}
        self.dnum = {s: 0 for s in self.dsem}
        self.waited = {s: {} for s in self.STREAMS}

    def add(self, stream, fn, reads=(), writes=(), dma=False):
        op = _Op(stream, fn, dma)
        idx = len(self.ops)
        self.ops.append(op)
        for regs, w in ((reads, False), (writes, True)):
            for r in regs:
                if r is None:
                    continue
                if not isinstance(r, tuple):
                    r = _region(r)
                self._access(idx, op, r, w)
        for d in list(op.deps):
            dop = self.ops[d]
            if dop.stream == stream and stream == "pe":
                op.deps.discard(d)
                continue
            if not dop.dma:
                dop.needs_inc = True
        return idx

    def _access(self, idx, op, r, w):
        key, plo, phi, lo, hi = r
        lst = self.track.get(key)
        if lst is None:
            lst = []
            self.track[key] = lst
        keep = []
        for e in lst:
            eplo, ephi, elo, ehi, eop, ew, estream, edma = e
            ov = eplo < phi and plo < ephi and elo < hi and lo < ehi
            if ov and (w or ew) and eop != idx:
                op.deps.add(eop)
            contained = ov and plo <= eplo and ephi <= phi and lo <= elo and ehi <= hi
            if contained and w:
                continue
            if contained and (not w) and (not ew) and estream == op.stream and not edma and not op.dma:
                continue
            keep.append(e)
        keep.append((plo, phi, lo, hi, idx, w, op.stream, op.dma))
        self.track[key] = keep

    def barrier(self):
        last = {}
        dmas = []
        for i, op in enumerate(self.ops):
            if op.fn is None:
                continue
            if op.dma:
                dmas.append(i)
            else:
                last[op.stream] = i
        for s in self.STREAMS:
            op = _Op(s, None, False)
            for st, i in last.items():
                if st != s:
                    op.deps.add(i)
                    self.ops[i].needs_inc = True
            for i in dmas:
                op.deps.add(i)
            self.ops.append(op)
        self.track = {}

    def emit(self):
        nc = self.nc
        ops = self.ops
        for op in ops:
            if op.fn is None:
                continue
            if op.dma:
                s = op.stream
                j = self.dnum[s] % self.KDMA
                self.dnum[s] += 1
                self.dcnt[s][j] += 16
                op.dsem = self.dsem[s][j]
                op.dval = self.dcnt[s][j]
            elif op.needs_inc:
                self.cnt[op.stream] += 1
                op.incval = self.cnt[op.stream]
        for op in ops:
            w = {}
            for d in op.deps:
                dop = ops[d]
                if dop.dma:
                    sem, val = dop.dsem, dop.dval
                else:
                    sem, val = self.sem[dop.stream], dop.incval
                k = id(sem)
                if k not in w or w[k][1] < val:
                    w[k] = (sem, val)
            if op.dma and op.dval > 16:
                k = id(op.dsem)
                v = op.dval - 16
                if k not in w or w[k][1] < v:
                    w[k] = (op.dsem, v)
            wd = self.waited[op.stream]
            out = []
            for k, (sem, val) in w.items():
                if wd.get(k, 0) >= val:
                    continue
                wd[k] = val
                out.append((sem, val))
            op.waits = out
        by = {s: [] for s in self.STREAMS}
        for op in ops:
            by[op.stream].append(op)
        sems = self.sem

        def run(stream, e):
            for op in by[stream]:
                for sem, val in op.waits:
                    e.wait_ge(sem, val)
                if op.fn is None:
                    continue
                ins = op.fn(e)
                if op.dma:
                    ins.then_inc(op.dsem, 16)
                elif op.needs_inc:
                    ins.then_inc(sems[stream], 1)

        with nc.Block() as block:
            @block.tensor
            def _(e):
                run("pe", e)

            @block.scalar
            def _(e):
                run("act", e)

            @block.vector
            def _(e):
                run("dve", e)

            @block.gpsimd
            def _(e):
                run("pool", e)

            @block.sync
            def _(e):
                run("sp", e)
        self.ops = []
        self.track = {}

    def mm(self, out, lhsT, rhs, start=True, stop=True):
        return self.add("pe", lambda e: e.matmul(out, lhsT=lhsT, rhs=rhs, start=start, stop=stop),
                        reads=[lhsT, rhs], writes=[out])

    def tr(self, out, in_, ident):
        return self.add("pe", lambda e: e.transpose(out, in_, ident), reads=[in_, ident], writes=[out])

    def act(self, out, in_, func, bias=None, scale=None):
        kw = {}
        rd = [in_]
        if bias is not None:
            kw["bias"] = bias
            if not isinstance(bias, (int, float)):
                rd.append(bias)
        if scale is not None:
            kw["scale"] = scale
            if not isinstance(scale, (int, float)):
                rd.append(scale)
        return self.add("act", lambda e: e.activation(out, in_, func, **kw), reads=rd, writes=[out])

    def tt(self, out, in0, in1, op, eng="dve"):
        return self.add(eng, lambda e: e.tensor_tensor(out, in0, in1, op), reads=[in0, in1], writes=[out])

    def ts(self, out, in0, s1, s2, op0, op1=None, eng="dve"):
        rd = [in0]
        for s in (s1, s2):
            if s is not None and not isinstance(s, (int, float)):
                rd.append(s)
        if op1 is None:
            return self.add(eng, lambda e: e.tensor_scalar(out, in0, s1, None, op0), reads=rd, writes=[out])
        return self.add(eng, lambda e: e.tensor_scalar(out, in0, s1, s2, op0, op1), reads=rd, writes=[out])

    def stt(self, out, in0, scalar, in1, op0, op1):
        rd = [in0, in1]
        if not isinstance(scalar, (int, float)):
            rd.append(scalar)
        return self.add("dve", lambda e: e.scalar_tensor_tensor(out, in0, scalar, in1, op0, op1),
                        reads=rd, writes=[out])

    def copy(self, out, in_, eng="dve"):
        return self.add(eng, lambda e: e.tensor_copy(out, in_), reads=[in_], writes=[out])

    def recip(self, out, in_):
        return self.add("dve", lambda e: e.reciprocal(out, in_), reads=[in_], writes=[out])

    def memset(self, ap, val, eng="dve"):
        return self.add(eng, lambda e: e.memset(ap, val), writes=[ap])

    def dma(self, out, in_, reads=(), writes=(), stream="sp"):
        rd = list(reads)
        wr = list(writes)
        if in_.tensor.__class__.__name__ != "DRamTensorHandle":
            rd.append(in_)
        if out.tensor.__class__.__name__ != "DRamTensorHandle":
            wr.append(out)
        return self.add(stream, lambda e: e.dma_start(out=out, in_=in_), reads=rd, writes=wr, dma=True)


def _cst_cols():
    cols = {}
    n = 0
    for l in range(DEPTH):
        for name, k in (("conv_b", 4), ("ln_g", 4), ("ln_b", 4), ("b_gate", 16), ("g_mix", 8),
                        ("g_mlp", 8), ("g_ple", 8), ("g_q", 2), ("g_kv", 1)):
            cols[(name, l)] = n
            n += k
    cols[("g_final", 0)] = n
    n += 8
    assert n <= NCST
    return cols


CST = _cst_cols()


def build_program(seqs):
    nc = bass.Bass("TRN2", target_bir_lowering=False)
    NS = len(seqs)
    SMAX = max(seqs)

    def din(name, shape, dt=F32):
        return nc.dram_tensor(name, list(shape), dt, kind="ExternalInput").ap()

    def dscr(name, shape, dt):
        return nc.dram_tensor(name, list(shape), dt, kind="Internal").ap()

    x_in = [din("x%d" % s, [seqs[s], D]) for s in range(NS)]
    p_in = [din("p%d" % s, [DEPTH, seqs[s], PLE]) for s in range(NS)]
    y_out = [nc.dram_tensor("y%d" % s, [seqs[s], D], F32, kind="ExternalOutput").ap() for s in range(NS)]
    w_p1 = din("w_p1", [DEPTH, D, 1472])
    w_g = din("w_g", [DEPTH, D, 2048])
    w_uq = din("w_uq", [DEPTH, QL, 1024])
    w_kv = din("w_kv", [DEPTH, KVL, 1024])
    w_oa = din("w_oa", [DEPTH, 512, D])
    w_pc = din("w_pc", [DEPTH, 512, D])
    w_out = din("w_out", [DEPTH, D, D])
    w_up = din("w_up", [DEPTH, D, DFF])
    w_down = din("w_down", [DEPTH, DFF, D])
    w_pg = din("w_pg", [DEPTH, D, D])
    w_ple = din("w_ple", [DEPTH, PLE, D])
    cw_in = din("cw", [DEPTH, 128, 4 * CW])
    cst_in = din("cst", [128, NCST])
    ident_in = din("ident", [128, 128])
    rope_in = din("rope", [128, 2, SMAX])

    xT = [dscr("xT%d" % s, [D, seqs[s]], F32) for s in range(NS)]
    qT = [dscr("qT%d" % s, [NH, 96, seqs[s]], BF16) for s in range(NS)]
    knT = [dscr("knT%d" % s, [NH, 64, seqs[s]], BF16) for s in range(NS)]
    krT = [dscr("krT%d" % s, [32, seqs[s]], BF16) for s in range(NS)]
    vS = [dscr("vS%d" % s, [seqs[s], NH * 65], BF16) for s in range(NS)]
    sT = [dscr("sT%d" % s, [512, seqs[s]], BF16) for s in range(NS)]
    gluS = [dscr("gluS%d" % s, [512, seqs[s] + 2 * PAD], BF16) for s in range(NS)]
    oT = [dscr("oT%d" % s, [512, seqs[s]], BF16) for s in range(NS)]

    with ExitStack() as es0:
        sc = Sched(nc, es0)

        uniq = [0]

        def sb(es, name, shape, dt):
            uniq[0] += 1
            return es.enter_context(nc.sbuf_tensor("%s_%d" % (name, uniq[0]), list(shape), dt))

        psum_all = es0.enter_context(nc.psum_tensor("psum_all", [128, 8 * 512], F32))

        class BV:
            def __init__(self, base, width=512):
                self.base = base
                self.width = width

            def __getitem__(self, idx):
                if not isinstance(idx, tuple):
                    idx = (idx, slice(None))
                rows, cols = idx
                c0 = cols.start or 0
                c1 = self.width if cols.stop is None else cols.stop
                return psum_all[rows, self.base + c0:self.base + c1]

        banks = [BV(i * 512) for i in range(8)]
        ones_bf = sb(es0, "ones_bf", [128, 128], BF16)
        ident = sb(es0, "ident", [128, 128], F32)
        cst = sb(es0, "cst", [128, NCST], F32)
        stg = [sb(es0, "stg%d" % i, [128, 2048], F32) for i in range(2)]
        state = {"bank": 0, "stg": 0, "cv": 0}

        def bank():
            b = banks[state["bank"] % 8]
            state["bank"] += 1
            return b

        def cc(name, l, j=0):
            c = CST[(name, l)] + j
            return cst[:, c:c + 1]

        sc.memset(ones_bf[:], 1.0)
        sc.dma(ident[:], ident_in[:, :])
        sc.dma(cst[:], cst_in[:, :])
        sc.barrier()
        sc.emit()

        def load_w(dst, src, K, N, scale_name=None, l=0, extra=None):
            KC = (K + 127) // 128
            for kc in range(KC):
                rows = min(128, K - kc * 128)
                for n0 in range(0, N, 2048):
                    n1 = min(N, n0 + 2048)
                    st = stg[state["stg"] % 2]
                    state["stg"] += 1
                    sc.dma(st[0:rows, 0:n1 - n0], src[kc * 128:kc * 128 + rows, n0:n1])
                    o = dst[0:rows, kc, n0:n1]
                    i_ = st[0:rows, 0:n1 - n0]
                    use_act = (state["cv"] % 2 == 1)
                    state["cv"] += 1
                    if scale_name is None:
                        if use_act:
                            sc.act(o, i_, AF.Copy)
                        else:
                            sc.copy(o, i_)
                    else:
                        g = cc(scale_name, l, kc)[0:rows, :]
                        if extra is not None:
                            sc.ts(o, i_, g, extra, ALU.mult, ALU.mult)
                        elif use_act:
                            sc.act(o, i_, AF.Copy, scale=g)
                        else:
                            sc.ts(o, i_, g, None, ALU.mult)

        def rstd_from(es_bufs, sq_chunks, nfeat, Tn):
            rt, rs = es_bufs
            ps = bank()
            n = len(sq_chunks)
            for i, q in enumerate(sq_chunks):
                sc.mm(ps[:, 0:Tn], ones_bf[:], q, start=(i == 0), stop=(i == n - 1))
            sc.act(rt[:, 0:Tn], ps[:, 0:Tn], AF.Sqrt, bias=epsc[:, 0:1], scale=1.0 / nfeat)
            sc.recip(rs[:, 0:Tn], rt[:, 0:Tn])
            return rs

        def scale_chunks(dst, x, r, Tn, n=8):
            for c in range(n):
                eng = "dve"
                sc.tt(dst[:, c, :], x[:, c, :], r[:, 0:Tn], ALU.mult, eng=eng)

        epsc = sb(es0, "epsc", [128, 1], F32)
        sc.memset(epsc[:], EPS)

        def xview(s, t0, Tn):
            return xT[s].rearrange("(c p) s -> p c s", p=128)[:, :, t0:t0 + Tn]

        for l in range(DEPTH):
            with ExitStack() as es:
                Wa = sb(es, "Wa", [128, 8, 1472], BF16)
                Wuq = sb(es, "Wuq", [128, 2, 1024], BF16)
                Wkv = sb(es, "Wkv", [128, 1, 1024], BF16)
                xtok = sb(es, "xtok", [128, 4, D], F32) if l == 0 else None
                xs = [sb(es, "xs%d" % i, [128, 8, T], F32) for i in range(2)]
                ropes = [sb(es, "rope%d" % i, [128, 2, T], F32) for i in range(3)]
                sq = sb(es, "sq", [128, 8, T], BF16)
                us = [sb(es, "u%d" % i, [128, 8, T], BF16) for i in range(2)]
                rt = sb(es, "rt", [128, T], F32)
                rs = sb(es, "rs", [128, T], F32)
                rt2 = rt
                rs2 = sb(es, "rs2", [128, T], F32)
                cq = sb(es, "cq", [128, 3, T], F32)
                cqsq = sb(es, "cqsq", [128, 3, T], BF16)
                cqn = sb(es, "cqn", [128, 3, T], BF16)
                t1 = sb(es, "t1", [128, T], F32)
                t2 = sb(es, "t2", [128, T], F32)
                tq = [sb(es, "tq%d" % i, [128, T], F32) for i in range(4)]
                krb = [sb(es, "krb%d" % i, [128, T], BF16) for i in range(2)]
                qn = sb(es, "qn", [128, 4, T], BF16)
                qr = sb(es, "qr", [128, 2, T], BF16)
                knb = [sb(es, "knb%d" % i, [128, 4, T], BF16) for i in range(1)]
                vb = [sb(es, "vb%d" % i, [128, 4, NH, 65], BF16) for i in range(1)]
                sig = [sb(es, "sig%d" % i, [128, T], F32) for i in range(2)]
                glu = [sb(es, "glu%d" % i, [128, 4, T], BF16) for i in range(2)]
                zpad = sb(es, "zpad", [128, 4, PAD], BF16)
                rs3 = sb(es, "rs3", [128, T], F32)

                load_w(Wa, w_p1[l], D, 1472, "g_mix", l)
                load_w(Wuq, w_uq[l], QL, 1024, "g_q", l, extra=ATTN_SCALE)
                load_w(Wkv, w_kv[l], KVL, 1024, "g_kv", l)
                sc.memset(zpad[:], 0.0)
                sc.memset(vb[0][:, :, :, 64:65], 1.0)
                for s_ in range(NS):
                    gv = gluS[s_].rearrange("(c p) s -> p c s", p=128)
                    sc.dma(gv[:, :, 0:PAD], zpad[:])
                    sc.dma(gv[:, :, PAD + seqs[s_]:2 * PAD + seqs[s_]], zpad[:])
                tiles = [(s_, i_) for s_ in range(NS) for i_ in range(seqs[s_] // T)]

                def p1_loads(n):
                    s_, i_ = tiles[n]
                    sc.dma(ropes[n % 3][:], rope_in[:, :, i_ * T:(i_ + 1) * T])
                    if l == 0:
                        sc.dma(xtok[:], x_in[s_][i_ * T:(i_ + 1) * T, :].rearrange("(s p) d -> p s d", p=128))
                    else:
                        sc.dma(xs[n % 2][:], xview(s_, i_ * T, T), reads=[("xT%d" % s_, 0, 1, i_, i_ + 1)])

                def p1_front(n):
                    s_, i_ = tiles[n]
                    x = xs[n % 2]
                    if l == 0:
                        for c in range(8):
                            ps = bank()
                            for k in range(4):
                                sc.tr(ps[:, k * 128:(k + 1) * 128], xtok[:, k, c * 128:(c + 1) * 128], ident[:])
                            if c % 2 == 0:
                                sc.copy(x[:, c, :], ps[:])
                            else:
                                sc.act(x[:, c, :], ps[:], AF.Copy)
                        if n + 1 < len(tiles):
                            p1_loads(n + 1)
                        sc.dma(xview(s_, i_ * T, T), x[:], writes=[("xT%d" % s_, 0, 1, i_, i_ + 1)])
                    else:
                        if n + 1 < len(tiles):
                            p1_loads(n + 1)
                    sc.act(sq[:], x[:], AF.Square)

                def p1_front2(n):
                    x = xs[n % 2]
                    r = rstd_from((rt, rs), [sq[:, c, :] for c in range(8)], D, T)
                    scale_chunks(us[n % 2], x, r, T)

                p1_loads(0)
                p1_front(0)
                p1_front2(0)
                nflat = 0
                for s in range(NS):
                    S = seqs[s]
                    NT = S // T

                    for i in range(NT):
                        t0 = i * T
                        x = xs[nflat % 2]
                        rp = ropes[nflat % 3]
                        u = us[nflat % 2]
                        nflat += 1

                        def zmm(ps_ap, col0, ncol):
                            for k in range(8):
                                sc.mm(ps_ap, Wa[:, k, col0:col0 + ncol], u[:, k, :], start=(k == 0), stop=(k == 7))

                        for j in range(3):
                            ps = bank()
                            zmm(ps[:], j * 128, 128)
                            sc.act(cq[:, j, :], ps[:], AF.Copy)
                            sc.act(cqsq[:, j, :], ps[:], AF.Square)
                        if nflat < len(tiles):
                            p1_front(nflat)
                        pa = bank()
                        pb = bank()
                        zmm(pa[64:96, :], 384, 32)
                        zmm(pb[64:96, :], 1440, 32)
                        rq = rstd_from((rt2, rs2), [cqsq[:, 0, :], cqsq[:, 1, :]], QL, T)
                        for j in range(2):
                            sc.tt(cqn[:, j, :], cq[:, j, :], rq[:], ALU.mult)
                        rk = rstd_from((rt, rs3), [cqsq[:, 2, :]], KVL, T)
                        sc.tt(cqn[:, 2, :], cq[:, 2, :], rk[:], ALU.mult)
                        kr_ = krb[i % 2]
                        sc.tt(t1[64:96, :], pa[64:96, :], rp[64:96, 0, :], ALU.mult)
                        sc.tt(t2[64:96, :], pb[64:96, :], rp[64:96, 1, :], ALU.mult)
                        sc.tt(kr_[64:96, :], t1[64:96, :], t2[64:96, :], ALU.add)
                        sc.dma(krT[s][:, t0:t0 + T], kr_[64:96, :], writes=[("krT%d" % s, 0, 1, i, i + 1)])
                        g = glu[i % 2]

                        def glu_chunk(c):
                            pa_ = bank()
                            pg_ = bank()
                            zmm(pa_[:], 416 + c * 128, 128)
                            zmm(pg_[:], 928 + c * 128, 128)
                            sg = sig[c % 2]
                            sc.act(sg[:], pg_[:], AF.Sigmoid)
                            sc.tt(g[:, c, :], pa_[:], sg[:], ALU.mult)

                        def q_nope(j):
                            pq = bank()
                            for kk in range(2):
                                sc.mm(pq[:], Wuq[:, kk, j * 128:(j + 1) * 128], cqn[:, kk, :],
                                      start=(kk == 0), stop=(kk == 1))
                            sc.act(qn[:, j, :], pq[:], AF.Copy)

                        def q_rope(r):
                            pr = bank()
                            pw = bank()
                            for kk in range(2):
                                sc.mm(pr[:], Wuq[:, kk, 512 + r * 128:512 + (r + 1) * 128], cqn[:, kk, :],
                                      start=(kk == 0), stop=(kk == 1))
                            for kk in range(2):
                                sc.mm(pw[:], Wuq[:, kk, 768 + r * 128:768 + (r + 1) * 128], cqn[:, kk, :],
                                      start=(kk == 0), stop=(kk == 1))
                            ta = tq[(2 * r) % 4]
                            tb = tq[(2 * r + 1) % 4]
                            sc.tt(ta[:], pr[:], rp[:, 0, :], ALU.mult)
                            sc.tt(tb[:], pw[:], rp[:, 1, :], ALU.mult)
                            sc.tt(qr[:, r, :], ta[:], tb[:], ALU.add)

                        glu_chunk(0)
                        glu_chunk(1)
                        q_nope(0)
                        q_rope(0)
                        glu_chunk(2)
                        q_nope(1)
                        q_rope(1)
                        glu_chunk(3)
                        q_nope(2)
                        q_nope(3)
                        for b_ in range(2):
                            sc.dma(qT[s].rearrange("(a b) r s -> b r a s", b=2)[b_][0:64, :, t0:t0 + T],
                                   qn[b_ * 64:(b_ + 1) * 64, :, :], writes=[("qT%d" % s, 0, 1, 8 * i + b_, 8 * i + b_ + 1)])
                        for m_ in range(4):
                            sc.dma(qT[s].rearrange("(r m) q s -> m q r s", m=4)[m_][64:96, :, t0:t0 + T],
                                   qr[m_ * 32:(m_ + 1) * 32, :, :],
                                   writes=[("qT%d" % s, 0, 1, 8 * i + 2 + m_, 8 * i + 3 + m_)])
                        sc.dma(gluS[s].rearrange("(c p) s -> p c s", p=128)[:, :, PAD + t0:PAD + t0 + T], g[:],
                               writes=[("gluS%d" % s, 0, 1, i, i + 1)])
                        if nflat < len(tiles):
                            p1_front2(nflat)
                        ko = knb[0]
                        for j in range(4):
                            ps = bank()
                            sc.mm(ps[:], Wkv[:, 0, j * 128:(j + 1) * 128], cqn[:, 2, :])
                            sc.act(ko[:, j, :], ps[:], AF.Copy)
                        sc.dma(knT[s].rearrange("(a b) r s -> (b r) a s", b=2)[:, :, t0:t0 + T], ko[:],
                               writes=[("knT%d" % s, 0, 1, i, i + 1)])
                        vo = vb[0]
                        for k in range(4):
                            ps = bank()
                            sc.mm(ps[:], cqn[:, 2, k * 128:(k + 1) * 128], Wkv[:, 0, 512:1024])
                            sc.act(vo[:, k, :, 0:64], ps[:].rearrange("p (h d) -> p h d", h=NH), AF.Copy)
                        sc.dma(vS[s][t0:t0 + T, :].rearrange("(k p) d -> p k d", p=128),
                               vo[:].rearrange("p k h d -> p k (h d)"),
                               writes=[("vS%d" % s, 0, 1, i, i + 1)])
                sc.barrier()
                sc.emit()

            with ExitStack() as es:
                SM = max(seqs)
                Kb = [sb(es, "Kb%d" % i, [128, SM], BF16) for i in range(2)]
                Vall = sb(es, "Vall", [128, SM // 128, NH * 65], BF16)
                Qb = [sb(es, "Qb%d" % i, [128, T], BF16) for i in range(3)]
                Pt = [sb(es, "Pt%d" % i, [128, 2 * T], BF16) for i in range(3)]
                oaug = [sb(es, "oaug%d" % i, [128, T], F32) for i in range(2)]
                rden = [sb(es, "rden%d" % i, [128, T], F32) for i in range(2)]
                obf = [sb(es, "obf%d" % i, [128, T], BF16) for i in range(2)]
                sel = sb(es, "sel", [128, 64], F32)
                sc.memset(sel[:], 0.0)
                sc.memset(sel[64:65, :], 1.0)
                cwt = sb(es, "cwt", [128, 4 * CW], F32)
                gin = [sb(es, "gin%d" % i, [128, 4, T + 2 * PAD], BF16) for i in range(2)]
                ysb = sb(es, "ysb", [128, 4, T], F32)
                ybf = sb(es, "ybf", [128, 4, T], BF16)
                ysq = sb(es, "ysq", [128, 4, T], BF16)
                mean = sb(es, "mean", [128, T], F32)
                msq = sb(es, "msq", [128, T], F32)
                var = sb(es, "var", [128, T], F32)
                rt3 = sb(es, "rt3", [128, T], F32)
                rs3 = sb(es, "rs3", [128, T], F32)
                tn = [sb(es, "tn%d" % i, [128, T], F32) for i in range(4)]
                sbf = [sb(es, "sbf%d" % i, [128, 4, T], BF16) for i in range(2)]
                sc.dma(cwt[:], cw_in[l])
                cstate = {"bank": None}

                def conv_gen():
                    ctiles = [(s_, j_) for s_ in range(NS) for j_ in range(seqs[s_] // T)]

                    def cload(n):
                        s_, j_ = ctiles[n]
                        sc.dma(gin[n % 2][:],
                               gluS[s_].rearrange("(c p) s -> p c s", p=128)[:, :, j_ * T:(j_ + 1) * T + 2 * PAD])

                    cload(0)
                    for n, (s_, j_) in enumerate(ctiles):
                        if n + 1 < len(ctiles):
                            cload(n + 1)
                        g = gin[n % 2]
                        for cp in range(2):
                            for tap in range(CW):
                                for c in (2 * cp, 2 * cp + 1):
                                    w = cwt[:, c * CW + tap:c * CW + tap + 1]
                                    if tap == 0:
                                        sc.ts(ysb[:, c, :], g[:, c, 0:T], w, cc("conv_b", l, c), ALU.mult, ALU.add)
                                    else:
                                        sc.stt(ysb[:, c, :], g[:, c, tap:tap + T], w, ysb[:, c, :], ALU.mult, ALU.add)
                                    yield
                            for c in (2 * cp, 2 * cp + 1):
                                sc.tt(ysq[:, c, :], ysb[:, c, :], ysb[:, c, :], ALU.mult)
                                yield
                                sc.copy(ybf[:, c, :], ysb[:, c, :])
                                yield
                        while cstate["bank"] is None:
                            yield
                        bk = cstate["bank"]
                        for c in range(4):
                            sc.mm(bk[:], ones_bf[:], ybf[:, c, :], start=(c == 0), stop=(c == 3))
                        sc.ts(mean[:], bk[:], 1.0 / CCH, None, ALU.mult)
                        for c in range(4):
                            sc.mm(bk[:], ones_bf[:], ysq[:, c, :], start=(c == 0), stop=(c == 3))
                        sc.tt(msq[:], mean[:], mean[:], ALU.mult)
                        sc.stt(var[:], bk[:], 1.0 / CCH, msq[:], ALU.mult, ALU.subtract)
                        sc.ts(var[:], var[:], 0.0, None, ALU.max)
                        for _ in range(8):
                            yield
                        sc.act(rt3[:], var[:], AF.Sqrt, bias=epsc[:, 0:1])
                        for _ in range(4):
                            yield
                        sc.recip(rs3[:], rt3[:])
                        yield
                        so = sbf[n % 2]
                        for c in range(4):
                            a_ = tn[c]
                            sc.tt(a_[:], ysb[:, c, :], mean[:], ALU.subtract)
                            yield
                            sc.tt(a_[:], a_[:], rs3[:], ALU.mult)
                            yield
                        for _ in range(8):
                            yield
                        for c in range(4):
                            sc.act(so[:, c, :], tn[c][:], AF.Silu, bias=cc("ln_b", l, c), scale=cc("ln_g", l, c))
                        sc.dma(sT[s_].rearrange("(c p) s -> p c s", p=128)[:, :, j_ * T:(j_ + 1) * T], so[:],
                               writes=[("sT%d" % s_, 0, 1, j_, j_ + 1)])
                        yield

                sgroups = [BV(0, 1024), BV(1024, 1024), BV(2048, 1024)]
                obanks = banks[6:8]
                dbanks = banks[6:8]
                nsc = 0
                iters = [(s, h, qi) for s in range(NS) for h in range(NH) for qi in range(seqs[s] // T)]

                def p2_loads(n):
                    s, h, qi = iters[n]
                    S = seqs[s]
                    NK = S // 128
                    if qi == 0:
                        if h < 2:
                            sc.dma(Kb[h % 2][64:96, 0:S], krT[s][:, :])
                        sc.dma(Kb[h % 2][0:64, 0:S], knT[s][h])
                    sc.dma(Qb[n % 3][0:96, :], qT[s][h][:, qi * T:(qi + 1) * T])

                pending = []

                def fin_b(it, s, h, qi):
                    oa = oaug[it % 2]
                    db = dbanks[it % 2]
                    sc.mm(db[0:64, :], sel[0:65, :], oa[0:65, :])
                    rd = rden[it % 2]
                    sc.recip(rd[0:64, :], db[0:64, :])
                    o_ = obf[it % 2]
                    sc.tt(o_[0:64, :], oa[0:64, :], rd[0:64, :], ALU.mult)
                    sc.dma(oT[s][h * 64:(h + 1) * 64, qi * T:(qi + 1) * T], o_[0:64, :],
                           writes=[("oT%d" % s, 0, 1, it, it + 1)])

                cg = conv_gen()

                def conv_pull(k=1):
                    for _ in range(k):
                        try:
                            next(cg)
                        except StopIteration:
                            break

                p2_loads(0)
                for it, (s, h, qi) in enumerate(iters):
                    S = seqs[s]
                    NK = S // 128
                    if h == 0 and qi == 0:
                        for k0 in range(0, NK, 8):
                            k1 = min(NK, k0 + 8)
                            sc.dma(Vall[:, k0:k1, :],
                                   vS[s][k0 * 128:k1 * 128, :].rearrange("(k p) d -> p k d", p=128))
                    if it + 1 < len(iters):
                        p2_loads(it + 1)
                    K_ = Kb[h % 2]
                    Q_ = Qb[it % 3]
                    ob = obanks[it % 2]
                    scs = {}
                    NG = NK // 2

                    def qk(g):
                        nonlocal nsc
                        grp = sgroups[nsc % 3]
                        nsc += 1
                        scs[g] = grp
                        for j in range(2):
                            kt = 2 * g + j
                            sc.mm(grp[:, j * T:(j + 1) * T], K_[0:96, kt * 128:(kt + 1) * 128], Q_[0:96, :])

                    qk(0)
                    if NG > 1:
                        qk(1)
                    for g in range(NG):
                        if g + 2 < NG:
                            qk(g + 2)
                        if g == 1 and pending:
                            fin_b(*pending.pop())
                        pt = Pt[g % 3]
                        sc.act(pt[:], scs.pop(g)[:], AF.Exp)
                        for j in range(2):
                            kt = 2 * g + j
                            sc.mm(ob[0:65, :], Vall[:, kt, h * 65:(h + 1) * 65], pt[:, j * T:(j + 1) * T],
                                  start=(kt == 0), stop=(kt == NK - 1))
                        cstate["bank"] = obanks[(it + 1) % 2] if (g >= 2 and not pending) else None
                        conv_pull(1)
                    cstate["bank"] = None
                    if pending:
                        fin_b(*pending.pop())
                    sc.copy(oaug[it % 2][0:65, :], ob[0:65, :])
                    pending.append((it, s, h, qi))
                if pending:
                    fin_b(*pending.pop())
                cstate["bank"] = obanks[len(iters) % 2]
                for _ in cg:
                    pass
                sc.barrier()
                sc.emit()

            with ExitStack() as es:
                Wg = sb(es, "Wg", [128, 8, 2048], BF16)
                Woa = sb(es, "Woa", [128, 4, D], BF16)
                Wpc = sb(es, "Wpc", [128, 4, D], BF16)
                Wo = sb(es, "Wo", [128, 8, D], BF16)
                xs = [sb(es, "xs%d" % i, [128, 8, T], F32) for i in range(2)]
                ss_ = [sb(es, "ss%d" % i, [128, 4, T], BF16) for i in range(2)]
                os_ = [sb(es, "os%d" % i, [128, 4, T], BF16) for i in range(2)]
                sq = sb(es, "sq", [128, 8, T], BF16)
                us = [sb(es, "u%d" % i, [128, 8, T], BF16) for i in range(2)]
                rt = sb(es, "rt", [128, T], F32)
                rs = sb(es, "rs", [128, T], F32)
                gA = [sb(es, "gA%d" % i, [128, T], F32) for i in range(2)]
                gC = [sb(es, "gC%d" % i, [128, T], F32) for i in range(2)]
                m1 = [sb(es, "m1%d" % i, [128, T], F32) for i in range(2)]
                m2 = [sb(es, "m2%d" % i, [128, T], F32) for i in range(2)]
                mg = sb(es, "mg", [128, 8, T], BF16)
                load_w(Wg, w_g[l], D, 2048, "g_mix", l)
                load_w(Woa, w_oa[l], 512, D)
                load_w(Wpc, w_pc[l], 512, D)
                load_w(Wo, w_out[l], D, D)
                tiles = [(s_, i_) for s_ in range(NS) for i_ in range(seqs[s_] // T)]

                def p3_loads(n):
                    s_, i_ = tiles[n]
                    sc.dma(xs[n % 2][:], xview(s_, i_ * T, T), reads=[("xT%d" % s_, 0, 1, i_, i_ + 1)])
                    sc.dma(ss_[n % 2][:], sT[s_].rearrange("(c p) s -> p c s", p=128)[:, :, i_ * T:(i_ + 1) * T])
                    sc.dma(os_[n % 2][:], oT[s_].rearrange("(c p) s -> p c s", p=128)[:, :, i_ * T:(i_ + 1) * T])

                def p3_front(n):
                    x = xs[n % 2]
                    sc.act(sq[:], x[:], AF.Square)
                    r = rstd_from((rt, rs), [sq[:, c, :] for c in range(8)], D, T)
                    scale_chunks(us[n % 2], x, r, T)

                p3_loads(0)
                p3_front(0)
                for n, (s, i) in enumerate(tiles):
                    if True:
                        t0 = i * T
                        x = xs[n % 2]
                        s_in = ss_[n % 2]
                        o_in = os_[n % 2]
                        u = us[n % 2]
                        if n + 1 < len(tiles):
                            p3_loads(n + 1)
                        for c in range(8):
                            if c == 5 and n + 1 < len(tiles):
                                p3_front(n + 1)
                            pA = bank()
                            pC = bank()
                            pa = bank()
                            pc_ = bank()
                            for k in range(8):
                                sc.mm(pA[:], Wg[:, k, c * 128:(c + 1) * 128], u[:, k, :], start=(k == 0), stop=(k == 7))
                            for k in range(8):
                                sc.mm(pC[:], Wg[:, k, D + c * 128:D + (c + 1) * 128], u[:, k, :],
                                      start=(k == 0), stop=(k == 7))
                            for k in range(4):
                                sc.mm(pa[:], Woa[:, k, c * 128:(c + 1) * 128], o_in[:, k, :], start=(k == 0), stop=(k == 3))
                            for k in range(4):
                                sc.mm(pc_[:], Wpc[:, k, c * 128:(c + 1) * 128], s_in[:, k, :], start=(k == 0), stop=(k == 3))
                            a = gA[c % 2]
                            b = gC[c % 2]
                            sc.act(a[:], pA[:], AF.Sigmoid, bias=cc("b_gate", l, c))
                            sc.act(b[:], pC[:], AF.Sigmoid, bias=cc("b_gate", l, 8 + c))
                            sc.tt(m1[c % 2][:], pa[:], a[:], ALU.mult)
                            sc.tt(m2[c % 2][:], pc_[:], b[:], ALU.mult)
                            sc.tt(mg[:, c, :], m1[c % 2][:], m2[c % 2][:], ALU.add)
                        for c in range(8):
                            ps = bank()
                            for k in range(8):
                                sc.mm(ps[:], Wo[:, k, c * 128:(c + 1) * 128], mg[:, k, :], start=(k == 0), stop=(k == 7))
                            sc.tt(x[:, c, :], ps[:], x[:, c, :], ALU.add)
                        sc.dma(xview(s, t0, T), x[:], writes=[("xT%d" % s, 0, 1, i, i + 1)])
                sc.barrier()
                sc.emit()

            with ExitStack() as es:
                T4 = 256
                Wu = sb(es, "Wu", [128, 8, DFF], BF16)
                Wd = sb(es, "Wd", [128, 32, D], BF16)
                xs = [sb(es, "xs%d" % i, [128, 8, T4], F32) for i in range(2)]
                sq = sb(es, "sq", [128, 8, T4], BF16)
                xns = [sb(es, "xn%d" % i, [128, 8, T4], BF16) for i in range(2)]
                rt = sb(es, "rt", [128, T4], F32)
                rs = sb(es, "rs", [128, T4], F32)
                rl = [sb(es, "rl%d" % i, [128, T4], F32) for i in range(3)]
                hb = sb(es, "hb", [128, 32, T4], BF16)
                load_w(Wu, w_up[l], D, DFF, "g_mlp", l)
                load_w(Wd, w_down[l], DFF, D)
                tiles = [(s_, i_) for s_ in range(NS) for i_ in range(seqs[s_] // T4)]

                def p4_loads(n):
                    s_, i_ = tiles[n]
                    sc.dma(xs[n % 2][:], xview(s_, i_ * T4, T4), reads=[("xT%d" % s_, 0, 1, i_ * T4, i_ * T4 + T4)])

                def p4_front(n):
                    x = xs[n % 2]
                    sc.act(sq[:], x[:], AF.Square)
                    r = rstd_from((rt, rs), [sq[:, c, :] for c in range(8)], D, T4)
                    scale_chunks(xns[n % 2], x, r, T4)

                p4_loads(0)
                p4_front(0)
                for n, (s, i) in enumerate(tiles):
                    if True:
                        t0 = i * T4
                        x = xs[n % 2]
                        xn = xns[n % 2]
                        if n + 1 < len(tiles):
                            p4_loads(n + 1)
                        for f in range(32):
                            if f == 20 and n + 1 < len(tiles):
                                p4_front(n + 1)
                            ps = bank()
                            for k in range(8):
                                sc.mm(ps[:, 0:T4], Wu[:, k, f * 128:(f + 1) * 128], xn[:, k, :],
                                      start=(k == 0), stop=(k == 7))
                            rr = rl[f % 3]
                            sc.act(rr[:], ps[:, 0:T4], AF.Relu)
                            sc.tt(hb[:, f, :], rr[:], rr[:], ALU.mult)
                        for c in range(8):
                            ps = bank()
                            for f in range(32):
                                sc.mm(ps[:, 0:T4], Wd[:, f, c * 128:(c + 1) * 128], hb[:, f, :],
                                      start=(f == 0), stop=(f == 31))
                            sc.tt(x[:, c, :], ps[:, 0:T4], x[:, c, :], ALU.add)
                        sc.dma(xview(s, t0, T4), x[:], writes=[("xT%d" % s, 0, 1, t0, t0 + T4)])
                sc.barrier()
                sc.emit()

            with ExitStack() as es:
                last = (l == DEPTH - 1)
                Wpg = sb(es, "Wpg", [128, 8, D], BF16)
                Wpl = sb(es, "Wpl", [128, 2, D], BF16)
                xs = [sb(es, "xs%d" % i, [128, 8, T], F32) for i in range(2)]
                ptok = [sb(es, "ptok%d" % i, [128, 4, PLE], F32) for i in range(2)]
                pTs = [sb(es, "pT%d" % i, [128, 2, T], BF16) for i in range(2)]
                xbs = [sb(es, "xb%d" % i, [128, 8, T], BF16) for i in range(2)]
                E = sb(es, "E", [128, 8, T], F32)
                esq = sb(es, "esq", [128, 8, T], BF16)
                rt = sb(es, "rt", [128, T], F32)
                rs = sb(es, "rs", [128, T], F32)
                rt2 = sb(es, "rt2", [128, T], F32)
                rs2 = sb(es, "rs2", [128, T], F32)
                sg = [sb(es, "sg%d" % i, [128, T], F32) for i in range(2)]
                en = [sb(es, "en%d" % i, [128, T], F32) for i in range(2)]
                yt = sb(es, "yt", [128, 8, T], F32) if last else None
                fsq = sb(es, "fsq", [128, 8, T], BF16) if last else None
                ytok = [sb(es, "ytok%d" % i, [128, D], F32) for i in range(2)] if last else None
                load_w(Wpg, w_pg[l], D, D)
                load_w(Wpl, w_ple[l], PLE, D)
                tiles = [(s_, i_) for s_ in range(NS) for i_ in range(seqs[s_] // T)]

                def p5_loads(n):
                    s_, i_ = tiles[n]
                    sc.dma(xs[n % 2][:], xview(s_, i_ * T, T), reads=[("xT%d" % s_, 0, 1, i_, i_ + 1)])
                    sc.dma(ptok[n % 2][:], p_in[s_][l][i_ * T:(i_ + 1) * T, :].rearrange("(s p) d -> p s d", p=128))

                def p5_front(n):
                    x = xs[n % 2]
                    pk = ptok[n % 2]
                    for j in range(2):
                        ps = bank()
                        for k in range(4):
                            sc.tr(ps[:, k * 128:(k + 1) * 128], pk[:, k, j * 128:(j + 1) * 128], ident[:])
                        sc.copy(pTs[n % 2][:, j, :], ps[:])
                    sc.act(xbs[n % 2][:, 0:4, :], x[:, 0:4, :], AF.Copy)
                    sc.copy(xbs[n % 2][:, 4:8, :], x[:, 4:8, :])

                fin_q = []
                fin_r = []
                ycnt = 0

                def final_a2():
                    n_, s_, i_ = fin_q[0]
                    x_ = xs[n_ % 2]
                    rf = rstd_from((rt2, rs2), [fsq[:, c, :] for c in range(8)], D, T)
                    for c in range(8):
                        sc.stt(yt[:, c, :], x_[:, c, :], cc("g_final", 0, c), rf[:], ALU.mult, ALU.mult)

                def final_b():
                    nonlocal ycnt
                    n_, s_, i_ = fin_q.pop(0)
                    for k in range(4):
                        yk = ytok[k % 2]
                        for half in range(2):
                            ps = bank()
                            for c4 in range(4):
                                c = half * 4 + c4
                                sc.tr(ps[:, c4 * 128:(c4 + 1) * 128], yt[:, c, k * 128:(k + 1) * 128], ident[:])
                            if half == 0:
                                sc.copy(yk[:, 0:512], ps[:])
                            else:
                                sc.act(yk[:, 512:1024], ps[:], AF.Copy)
                        ycnt += 1
                        sc.dma(y_out[s_][i_ * T + k * 128:i_ * T + (k + 1) * 128, :], yk[:],
                               writes=[("y%d" % s_, 0, 1, ycnt, ycnt + 1)])

                p5_loads(0)
                p5_front(0)
                for n, (s, i) in enumerate(tiles):
                    if True:
                        t0 = i * T
                        x = xs[n % 2]
                        pk = ptok[n % 2]
                        pT = pTs[n % 2]
                        xb = xbs[n % 2]
                        for c in range(8):
                            ps = bank()
                            for j in range(2):
                                sc.mm(ps[:], Wpl[:, j, c * 128:(c + 1) * 128], pT[:, j, :], start=(j == 0), stop=(j == 1))
                            sc.act(E[:, c, :], ps[:], AF.Copy)
                            sc.act(esq[:, c, :], ps[:], AF.Square)
                        if last and fin_q:
                            final_a2()
                        if n + 1 < len(tiles):
                            p5_loads(n + 1)
                        r = rstd_from((rt, rs), [esq[:, c, :] for c in range(8)], D, T)
                        for c in range(8):
                            if c == 4 and n + 1 < len(tiles):
                                p5_front(n + 1)
                            if c == 2 and last and fin_q:
                                final_b()
                            ps = bank()
                            for k in range(8):
                                sc.mm(ps[:], Wpg[:, k, c * 128:(c + 1) * 128], xb[:, k, :], start=(k == 0), stop=(k == 7))
                            g_ = sg[c % 2]
                            e_ = en[c % 2]
                            sc.act(g_[:], ps[:], AF.Sigmoid)
                            sc.stt(e_[:], E[:, c, :], cc("g_ple", l, c), r[:], ALU.mult, ALU.mult)
                            sc.tt(e_[:], e_[:], g_[:], ALU.mult)
                            sc.tt(x[:, c, :], x[:, c, :], e_[:], ALU.add)
                        if not last:
                            sc.dma(xview(s, t0, T), x[:], writes=[("xT%d" % s, 0, 1, i, i + 1)])
                        else:
                            sc.act(fsq[:], x[:], AF.Square)
                            fin_q.append((n, s, i))
                if last:
                    while fin_q:
                        final_a2()
                        final_b()
                sc.barrier()
                sc.emit()
    return nc


def _rope_tables(smax):
    inv = (1.0 / (np.float32(10000.0) ** (np.arange(0, ROPE, 2, dtype=np.float32) / np.float32(ROPE)))).astype(np.float32)
    ang = (np.arange(smax, dtype=np.float32)[:, None] * inv[None, :]).astype(np.float32)
    cos = np.cos(ang).astype(np.float32).T
    sin = np.sin(ang).astype(np.float32).T
    tab = np.zeros((32, 2, smax), np.float32)
    tab[0:16, 0] = cos
    tab[16:32, 0] = cos
    tab[0:16, 1] = -sin
    tab[16:32, 1] = sin
    return np.ascontiguousarray(np.tile(tab, (4, 1, 1)))


def _prep_shared(inp, smax):
    f = lambda a: np.ascontiguousarray(np.asarray(a, dtype=np.float32))
    w_in = f(inp["w_in"])
    krsw = np.concatenate([w_in[:, :, 400:416], w_in[:, :, 384:400]], axis=2)
    w_p1 = np.concatenate([w_in[:, :, 0:1440], krsw], axis=2)
    w_g = w_in[:, :, 1440:3488]
    wuq = f(inp["w_uq"])
    nop, rop, sw = [], [], []
    for h in range(NH):
        b = h * 96 + 64
        nop.append(wuq[:, :, h * 96:h * 96 + 64])
        rop.append(wuq[:, :, b:b + 32])
        sw.append(wuq[:, :, b + 16:b + 32])
        sw.append(wuq[:, :, b:b + 16])
    w_uq = np.concatenate(nop + rop + sw, axis=2)
    wkv = f(inp["w_ukv"]).reshape(DEPTH, KVL, NH, 128)
    w_kv = np.concatenate([wkv[:, :, :, 0:64].reshape(DEPTH, KVL, 512),
                           wkv[:, :, :, 64:128].reshape(DEPTH, KVL, 512)], axis=2)
    cwv = f(inp["conv_w"])
    cw = cwv.reshape(DEPTH, CW, 4, 128).transpose(0, 3, 2, 1).reshape(DEPTH, 128, 4 * CW)
    cst = np.zeros((128, NCST), np.float32)
    for (name, l), c0 in CST.items():
        v = f(inp[name])
        v = v if name == "g_final" else v[l]
        k = v.shape[0] // 128
        cst[:, c0:c0 + k] = v.reshape(k, 128).T
    return {
        "w_p1": f(w_p1), "w_g": f(w_g), "w_uq": f(w_uq), "w_kv": f(w_kv),
        "w_oa": f(inp["w_oa"]), "w_pc": f(inp["w_pc"]), "w_out": f(inp["w_out"]),
        "w_up": f(inp["w_up"]), "w_down": f(inp["w_down"]), "w_pg": f(inp["w_ple_gate"]),
        "w_ple": f(inp["w_ple"]), "cw": f(cw), "cst": cst,
        "ident": np.eye(128, dtype=np.float32), "rope": _rope_tables(smax),
    }


def run_cores(inp, n_cores=8):
    xp = np.asarray(inp["x_prompt"], np.float32)
    xsm = np.asarray(inp["x_sample"], np.float32)
    pp = np.asarray(inp["p_prompt"], np.float32)
    psm = np.asarray(inp["p_sample"], np.float32)
    S0, S1 = xp.shape[1], xsm.shape[1]
    nb0, nb1 = xp.shape[0], xsm.shape[0]
    nc = build_program([S0, S1])
    shared = _prep_shared(inp, max(S0, S1))
    in_maps = []
    for c in range(n_cores):
        m = dict(shared)
        b0 = c % nb0
        b1 = c % nb1
        m["x0"] = np.ascontiguousarray(xp[b0])
        m["x1"] = np.ascontiguousarray(xsm[b1])
        m["p0"] = np.ascontiguousarray(pp[:, b0])
        m["p1"] = np.ascontiguousarray(psm[:, b1])
        in_maps.append(m)
    res = run_bass_kernel_spmd(nc, in_maps, core_ids=list(range(n_cores)))
    yp = np.stack([np.asarray(res.results[b % n_cores]["y0"], np.float32) for b in range(nb0)], axis=0)
    ys = np.stack([np.asarray(res.results[b % n_cores]["y1"], np.float32) for b in range(nb1)], axis=0)
    return yp, ys


def kernel(**inputs):
    yp, ys = run_cores(inputs, 8)
    return (yp, ys)
```

```python
import numpy as np
from contextlib import ExitStack
import concourse.bass as bass
import concourse.mybir as mybir
from concourse.bass_utils import run_bass_kernel_spmd

F32 = mybir.dt.float32
BF16 = mybir.dt.bfloat16
AF = mybir.ActivationFunctionType
ALU = mybir.AluOpType

D = 1024
DEPTH = 2
NH = 8
QL = 256
KVL = 128
NOPE = 64
ROPE = 32
VH = 64
CCH = 512
CW = 31
PAD = 15
DFF = 4096
PLE = 256
EPS = 1e-6
ATTN_SCALE = float((NOPE + ROPE) ** -0.5)
T = 512
NCST = 128


class _Op:
    __slots__ = ("stream", "fn", "deps", "dma", "needs_inc", "incval", "dsem", "dval", "waits")

    def __init__(self, stream, fn, dma):
        self.stream = stream
        self.fn = fn
        self.deps = set()
        self.dma = dma
        self.needs_inc = False
        self.incval = None
        self.dsem = None
        self.dval = None
        self.waits = None


def _region(ap):
    t = ap.tensor
    shp = list(t.shape)
    fsz = 1
    for d in shp[1:]:
        fsz *= int(d)
    off = int(ap.offset)
    dims = [(int(s), int(c)) for s, c in ap.ap]
    p0 = off // fsz
    f0 = off % fsz
    if dims and dims[0][0] == fsz:
        pc = dims[0][1]
        rest = dims[1:]
    else:
        pc = 1
        rest = dims
    span = 1
    for s, c in rest:
        span += (c - 1) * abs(s)
    return (t.name, p0, p0 + pc, f0, f0 + span)


class Sched:
    STREAMS = ("pe", "act", "dve", "pool", "sp")
    KDMA = 8

    def __init__(self, nc, es):
        self.nc = nc
        self.ops = []
        self.track = {}
        self.sem = {s: es.enter_context(nc.semaphore("s_" + s)) for s in ("pe", "act", "dve", "pool")}
        self.dsem = {s: [es.enter_context(nc.semaphore("d_%s%d" % (s, j))) for j in range(self.KDMA)]
                     for s in ("sp", "pool")}
        self.cnt = {s: 0 for s in self.sem}
        self.dcnt = {s: [0] * self.KDMA for s in self.dsem}
        self.dnum = {s: 0 for s in self.dsem}
        self.waited = {s: {} for s in self.STREAMS}

    def add(self, stream, fn, reads=(), writes=(), dma=False):
        op = _Op(stream, fn, dma)
        idx = len(self.ops)
        self.ops.append(op)
        for regs, w in ((reads, False), (writes, True)):
            for r in regs:
                if r is None:
                    continue
                if not isinstance(r, tuple):
                    r = _region(r)
                self._access(idx, op, r, w)
        for d in list(op.deps):
            dop = self.ops[d]
            if dop.stream == stream and stream == "pe":
                op.deps.discard(d)
                continue
            if not dop.dma:
                dop.needs_inc = True
        return idx

    def _access(self, idx, op, r, w):
        key, plo, phi, lo, hi = r
        lst = self.track.get(key)
        if lst is None:
            lst = []
            self.track[key] = lst
        keep = []
        for e in lst:
            eplo, ephi, elo, ehi, eop, ew, estream, edma = e
            ov = eplo < phi and plo < ephi and elo < hi and lo < ehi
            if ov and (w or ew) and eop != idx:
                op.deps.add(eop)
            contained = ov and plo <= eplo and ephi <= phi and lo <= elo and ehi <= hi
            if contained and w:
                continue
            if contained and (not w) and (not ew) and estream == op.stream and not edma and not op.dma:
                continue
            keep.append(e)
        keep.append((plo, phi, lo, hi, idx, w, op.stream, op.dma))
        self.track[key] = keep

    def barrier(self):
        last = {}
        dmas = []
        for i, op in enumerate(self.ops):
            if op.fn is None:
                continue
            if op.dma:
                dmas.append(i)
            else:
                last[op.stream] = i
        for s in self.STREAMS:
            op = _Op(s, None, False)
            for st, i in last.items():
                if st != s:
                    op.deps.add(i)
                    self.ops[i].needs_inc = True
            for i in dmas:
                op.deps.add(i)
            self.ops.append(op)
        self.track = {}

    def emit(self):
        nc = self.nc
        ops = self.ops
        for op in ops:
            if op.fn is None:
                continue
            if op.dma:
                s = op.stream
                j = self.dnum[s] % self.KDMA
                self.dnum[s] += 1
                self.dcnt[s][j] += 16
                op.dsem = self.dsem[s][j]
                op.dval = self.dcnt[s][j]
            elif op.needs_inc:
                self.cnt[op.stream] += 1
                op.incval = self.cnt[op.stream]
        for op in ops:
            w = {}
            for d in op.deps:
                dop = ops[d]
                if dop.dma:
                    sem, val = dop.dsem, dop.dval
                else:
                    sem, val = self.sem[dop.stream], dop.incval
                k = id(sem)
                if k not in w or w[k][1] < val:
                    w[k] = (sem, val)
            if op.dma and op.dval > 16:
                k = id(op.dsem)
                v = op.dval - 16
                if k not in w or w[k][1] < v:
                    w[k] = (op.dsem, v)
            wd = self.waited[op.stream]
            out = []
            for k, (sem, val) in w.items():
                if wd.get(k, 0) >= val:
                    continue
                wd[k] = val
                out.append((sem, val))
            op.waits = out
        by = {s: [] for s in self.STREAMS}
        for op in ops:
            by[op.stream].append(op)
        sems = self.sem

        def run(stream, e):
            for op in by[stream]:
                for sem, val in op.waits:
                    e.wait_ge(sem, val)
                if op.fn is None:
                    continue
                ins = op.fn(e)
                if op.dma:
                    ins.then_inc(op.dsem, 16)
                elif op.needs_inc:
                    ins.then_inc(sems[stream], 1)

        with nc.Block() as block:
            @block.tensor
            def _(e):
                run("pe", e)

            @block.scalar
            def _(e):
                run("act", e)

            @block.vector
            def _(e):
                run("dve", e)

            @block.gpsimd
            def _(e):
                run("pool", e)

            @block.sync
            def _(e):
                run("sp", e)
        self.ops = []
        self.track = {}

    def mm(self, out, lhsT, rhs, start=True, stop=True):
        return self.add("pe", lambda e: e.matmul(out, lhsT=lhsT, rhs=rhs, start=start, stop=stop),
                        reads=[lhsT, rhs], writes=[out])

    def tr(self, out, in_, ident):
        return self.add("pe", lambda e: e.transpose(out, in_, ident), reads=[in_, ident], writes=[out])

    def act(self, out, in_, func, bias=None, scale=None):
        kw = {}
        rd = [in_]
        if bias is not None:
            kw["bias"] = bias
            if not isinstance(bias, (int, float)):
                rd.append(bias)
        if scale is not None:
            kw["scale"] = scale
            if not isinstance(scale, (int, float)):
                rd.append(scale)
        return self.add("act", lambda e: e.activation(out, in_, func, **kw), reads=rd, writes=[out])

    def tt(self, out, in0, in1, op, eng="dve"):
        return self.add(eng, lambda e: e.tensor_tensor(out, in0, in1, op), reads=[in0, in1], writes=[out])

    def ts(self, out, in0, s1, s2, op0, op1=None, eng="dve"):
        rd = [in0]
        for s in (s1, s2):
            if s is not None and not isinstance(s, (int, float)):
                rd.append(s)
        if op1 is None:
            return self.add(eng, lambda e: e.tensor_scalar(out, in0, s1, None, op0), reads=rd, writes=[out])
        return self.add(eng, lambda e: e.tensor_scalar(out, in0, s1, s2, op0, op1), reads=rd, writes=[out])

    def stt(self, out, in0, scalar, in1, op0, op1):
        rd = [in0, in1]
        if not isinstance(scalar, (int, float)):
            rd.append(scalar)
        return self.add("dve", lambda e: e.scalar_tensor_tensor(out, in0, scalar, in1, op0, op1),
                        reads=rd, writes=[out])

    def copy(self, out, in_, eng="dve"):
        return self.add(eng, lambda e: e.tensor_copy(out, in_), reads=[in_], writes=[out])

    def recip(self, out, in_):
        return self.add("dve", lambda e: e.reciprocal(out, in_), reads=[in_], writes=[out])

    def memset(self, ap, val, eng="dve"):
        return self.add(eng, lambda e: e.memset(ap, val), writes=[ap])

    def dma(self, out, in_, reads=(), writes=(), stream="sp"):
        rd = list(reads)
        wr = list(writes)
        if in_.tensor.__class__.__name__ != "DRamTensorHandle":
            rd.append(in_)
        if out.tensor.__class__.__name__ != "DRamTensorHandle":
            wr.append(out)
        return self.add(stream, lambda e: e.dma_start(out=out, in_=in_), reads=rd, writes=wr, dma=True)


def _cst_cols():
    cols = {}
    n = 0
    for l in range(DEPTH):
        for name, k in (("conv_b", 4), ("ln_g", 4), ("ln_b", 4), ("b_gate", 16), ("g_mix", 8),
                        ("g_mlp", 8), ("g_ple", 8), ("g_q", 2), ("g_kv", 1)):
            cols[(name, l)] = n
            n += k
    cols[("g_final", 0)] = n
    n += 8
    assert n <= NCST
    return cols


CST = _cst_cols()


def build_program(seqs):
    nc = bass.Bass("TRN2", target_bir_lowering=False)
    NS = len(seqs)
    SMAX = max(seqs)

    def din(name, shape, dt=F32):
        return nc.dram_tensor(name, list(shape), dt, kind="ExternalInput").ap()

    def dscr(name, shape, dt):
        return nc.dram_tensor(name, list(shape), dt, kind="Internal").ap()

    x_in = [din("x%d" % s, [seqs[s], D]) for s in range(NS)]
    p_in = [din("p%d" % s, [DEPTH, seqs[s], PLE]) for s in range(NS)]
    y_out = [nc.dram_tensor("y%d" % s, [seqs[s], D], F32, kind="ExternalOutput").ap() for s in range(NS)]
    w_p1 = din("w_p1", [DEPTH, D, 1472])
    w_g = din("w_g", [DEPTH, D, 2048])
    w_uq = din("w_uq", [DEPTH, QL, 1024])
    w_kv = din("w_kv", [DEPTH, KVL, 1024])
    w_oa = din("w_oa", [DEPTH, 512, D])
    w_pc = din("w_pc", [DEPTH, 512, D])
    w_out = din("w_out", [DEPTH, D, D])
    w_up = din("w_up", [DEPTH, D, DFF])
    w_down = din("w_down", [DEPTH, DFF, D])
    w_pg = din("w_pg", [DEPTH, D, D])
    w_ple = din("w_ple", [DEPTH, PLE, D])
    cw_in = din("cw", [DEPTH, 128, 4 * CW])
    cst_in = din("cst", [128, NCST])
    ident_in = din("ident", [128, 128])
    rope_in = din("rope", [128, 2, SMAX])

    xT = [dscr("xT%d" % s, [D, seqs[s]], F32) for s in range(NS)]
    qT = [dscr("qT%d" % s, [NH, 96, seqs[s]], BF16) for s in range(NS)]
    knT = [dscr("knT%d" % s, [NH, 64, seqs[s]], BF16) for s in range(NS)]
    krT = [dscr("krT%d" % s, [32, seqs[s]], BF16) for s in range(NS)]
    vS = [dscr("vS%d" % s, [seqs[s], NH * 65], BF16) for s in range(NS)]
    sT = [dscr("sT%d" % s, [512, seqs[s]], BF16) for s in range(NS)]
    gluS = [dscr("gluS%d" % s, [512, seqs[s] + 2 * PAD], BF16) for s in range(NS)]
    oT = [dscr("oT%d" % s, [512, seqs[s]], BF16) for s in range(NS)]

    with ExitStack() as es0:
        sc = Sched(nc, es0)

        uniq = [0]

        def sb(es, name, shape, dt):
            uniq[0] += 1
            return es.enter_context(nc.sbuf_tensor("%s_%d" % (name, uniq[0]), list(shape), dt))

        psum_all = es0.enter_context(nc.psum_tensor("psum_all", [128, 8 * 512], F32))

        class BV:
            def __init__(self, base, width=512):
                self.base = base
                self.width = width

            def __getitem__(self, idx):
                if not isinstance(idx, tuple):
                    idx = (idx, slice(None))
                rows, cols = idx
                c0 = cols.start or 0
                c1 = self.width if cols.stop is None else cols.stop
                return psum_all[rows, self.base + c0:self.base + c1]

        banks = [BV(i * 512) for i in range(8)]
        ones_bf = sb(es0, "ones_bf", [128, 128], BF16)
        ident = sb(es0, "ident", [128, 128], F32)
        cst = sb(es0, "cst", [128, NCST], F32)
        stg = [sb(es0, "stg%d" % i, [128, 2048], F32) for i in range(2)]
        state = {"bank": 0, "stg": 0, "cv": 0}

        def bank():
            b = banks[state["bank"] % 8]
            state["bank"] += 1
            return b

        def cc(name, l, j=0):
            c = CST[(name, l)] + j
            return cst[:, c:c + 1]

        sc.memset(ones_bf[:], 1.0)
        sc.dma(ident[:], ident_in[:, :])
        sc.dma(cst[:], cst_in[:, :])
        sc.barrier()
        sc.emit()

        def load_w(dst, src, K, N, scale_name=None, l=0, extra=None):
            KC = (K + 127) // 128
            for kc in range(KC):
                rows = min(128, K - kc * 128)
                for n0 in range(0, N, 2048):
                    n1 = min(N, n0 + 2048)
                    st = stg[state["stg"] % 2]
                    state["stg"] += 1
                    sc.dma(st[0:rows, 0:n1 - n0], src[kc * 128:kc * 128 + rows, n0:n1])
                    o = dst[0:rows, kc, n0:n1]
                    i_ = st[0:rows, 0:n1 - n0]
                    use_act = (state["cv"] % 2 == 1)
                    state["cv"] += 1
                    if scale_name is None:
                        if use_act:
                            sc.act(o, i_, AF.Copy)
                        else:
                            sc.copy(o, i_)
                    else:
                        g = cc(scale_name, l, kc)[0:rows, :]
                        if extra is not None:
                            sc.ts(o, i_, g, extra, ALU.mult, ALU.mult)
                        elif use_act:
                            sc.act(o, i_, AF.Copy, scale=g)
                        else:
                            sc.ts(o, i_, g, None, ALU.mult)

        def rstd_from(es_bufs, sq_chunks, nfeat, Tn):
            rt, rs = es_bufs
            ps = bank()
            n = len(sq_chunks)
            for i, q in enumerate(sq_chunks):
                sc.mm(ps[:, 0:Tn], ones_bf[:], q, start=(i == 0), stop=(i == n - 1))
            sc.act(rt[:, 0:Tn], ps[:, 0:Tn], AF.Sqrt, bias=epsc[:, 0:1], scale=1.0 / nfeat)
            sc.recip(rs[:, 0:Tn], rt[:, 0:Tn])
            return rs

        def scale_chunks(dst, x, r, Tn, n=8):
            for c in range(n):
                eng = "pool" if c in (2, 6) else "dve"
                sc.tt(dst[:, c, :], x[:, c, :], r[:, 0:Tn], ALU.mult, eng=eng)

        epsc = sb(es0, "epsc", [128, 1], F32)
        sc.memset(epsc[:], EPS)

        def xview(s, t0, Tn):
            return xT[s].rearrange("(c p) s -> p c s", p=128)[:, :, t0:t0 + Tn]

        for l in range(DEPTH):
            with ExitStack() as es:
                Wa = sb(es, "Wa", [128, 8, 1472], BF16)
                Wuq = sb(es, "Wuq", [128, 2, 1024], BF16)
                Wkv = sb(es, "Wkv", [128, 1, 1024], BF16)
                xtok = sb(es, "xtok", [128, 4, D], F32) if l == 0 else None
                xs = [sb(es, "xs%d" % i, [128, 8, T], F32) for i in range(2)]
                ropes = [sb(es, "rope%d" % i, [128, 2, T], F32) for i in range(3)]
                sq = sb(es, "sq", [128, 8, T], BF16)
                us = [sb(es, "u%d" % i, [128, 8, T], BF16) for i in range(2)]
                rt = sb(es, "rt", [128, T], F32)
                rs = sb(es, "rs", [128, T], F32)
                rt2 = rt
                rs2 = sb(es, "rs2", [128, T], F32)
                cq = sb(es, "cq", [128, 3, T], F32)
                cqsq = sb(es, "cqsq", [128, 3, T], BF16)
                cqn = sb(es, "cqn", [128, 3, T], BF16)
                t1 = sb(es, "t1", [128, T], F32)
                t2 = sb(es, "t2", [128, T], F32)
                tq = [sb(es, "tq%d" % i, [128, T], F32) for i in range(4)]
                krb = [sb(es, "krb%d" % i, [128, T], BF16) for i in range(2)]
                qn = sb(es, "qn", [128, 4, T], BF16)
                qr = sb(es, "qr", [128, 2, T], BF16)
                knb = [sb(es, "knb%d" % i, [128, 4, T], BF16) for i in range(1)]
                vb = [sb(es, "vb%d" % i, [128, 4, NH, 65], BF16) for i in range(1)]
                sig = [sb(es, "sig%d" % i, [128, T], F32) for i in range(2)]
                glu = [sb(es, "glu%d" % i, [128, 4, T], BF16) for i in range(2)]
                zpad = sb(es, "zpad", [128, 4, PAD], BF16)
                rs3 = sb(es, "rs3", [128, T], F32)

                load_w(Wa, w_p1[l], D, 1472, "g_mix", l)
                load_w(Wuq, w_uq[l], QL, 1024, "g_q", l, extra=ATTN_SCALE)
                load_w(Wkv, w_kv[l], KVL, 1024, "g_kv", l)
                sc.memset(zpad[:], 0.0)
                sc.memset(vb[0][:, :, :, 64:65], 1.0)
                for s_ in range(NS):
                    gv = gluS[s_].rearrange("(c p) s -> p c s", p=128)
                    sc.dma(gv[:, :, 0:PAD], zpad[:])
                    sc.dma(gv[:, :, PAD + seqs[s_]:2 * PAD + seqs[s_]], zpad[:])
                tiles = [(s_, i_) for s_ in range(NS) for i_ in range(seqs[s_] // T)]

                def p1_loads(n):
                    s_, i_ = tiles[n]
                    sc.dma(ropes[n % 3][:], rope_in[:, :, i_ * T:(i_ + 1) * T])
                    if l == 0:
                        sc.dma(xtok[:], x_in[s_][i_ * T:(i_ + 1) * T, :].rearrange("(s p) d -> p s d", p=128))
                    else:
                        sc.dma(xs[n % 2][:], xview(s_, i_ * T, T), reads=[("xT%d" % s_, 0, 1, i_, i_ + 1)])

                def p1_front(n):
                    s_, i_ = tiles[n]
                    x = xs[n % 2]
                    if l == 0:
                        for c in range(8):
                            ps = bank()
                            for k in range(4):
                                sc.tr(ps[:, k * 128:(k + 1) * 128], xtok[:, k, c * 128:(c + 1) * 128], ident[:])
                            if c % 2 == 0:
                                sc.copy(x[:, c, :], ps[:])
                            else:
                                sc.act(x[:, c, :], ps[:], AF.Copy)
                        if n + 1 < len(tiles):
                            p1_loads(n + 1)
                        sc.dma(xview(s_, i_ * T, T), x[:], writes=[("xT%d" % s_, 0, 1, i_, i_ + 1)])
                    else:
                        if n + 1 < len(tiles):
                            p1_loads(n + 1)
                    sc.act(sq[:], x[:], AF.Square)

                def p1_front2(n):
                    x = xs[n % 2]
                    r = rstd_from((rt, rs), [sq[:, c, :] for c in range(8)], D, T)
                    scale_chunks(us[n % 2], x, r, T)

                p1_loads(0)
                p1_front(0)
                p1_front2(0)
                nflat = 0
                for s in range(NS):
                    S = seqs[s]
                    NT = S // T

                    for i in range(NT):
                        t0 = i * T
                        x = xs[nflat % 2]
                        rp = ropes[nflat % 3]
                        u = us[nflat % 2]
                        nflat += 1

                        def zmm(ps_ap, col0, ncol):
                            for k in range(8):
                                sc.mm(ps_ap, Wa[:, k, col0:col0 + ncol], u[:, k, :], start=(k == 0), stop=(k == 7))

                        for j in range(3):
                            ps = bank()
                            zmm(ps[:], j * 128, 128)
                            sc.act(cq[:, j, :], ps[:], AF.Copy)
                            sc.act(cqsq[:, j, :], ps[:], AF.Square)
                        if nflat < len(tiles):
                            p1_front(nflat)
                        pa = bank()
                        pb = bank()
                        zmm(pa[64:96, :], 384, 32)
                        zmm(pb[64:96, :], 1440, 32)
                        rq = rstd_from((rt2, rs2), [cqsq[:, 0, :], cqsq[:, 1, :]], QL, T)
                        for j in range(2):
                            sc.tt(cqn[:, j, :], cq[:, j, :], rq[:], ALU.mult, eng=("pool" if j == 1 else "dve"))
                        rk = rstd_from((rt, rs3), [cqsq[:, 2, :]], KVL, T)
                        sc.tt(cqn[:, 2, :], cq[:, 2, :], rk[:], ALU.mult)
                        kr_ = krb[i % 2]
                        sc.tt(t1[64:96, :], pa[64:96, :], rp[64:96, 0, :], ALU.mult)
                        sc.tt(t2[64:96, :], pb[64:96, :], rp[64:96, 1, :], ALU.mult)
                        sc.tt(kr_[64:96, :], t1[64:96, :], t2[64:96, :], ALU.add, eng="pool")
                        sc.dma(krT[s][:, t0:t0 + T], kr_[64:96, :], writes=[("krT%d" % s, 0, 1, i, i + 1)])
                        g = glu[i % 2]

                        def glu_chunk(c):
                            pa_ = bank()
                            pg_ = bank()
                            zmm(pa_[:], 416 + c * 128, 128)
                            zmm(pg_[:], 928 + c * 128, 128)
                            sg = sig[c % 2]
                            sc.act(sg[:], pg_[:], AF.Sigmoid)
                            sc.tt(g[:, c, :], pa_[:], sg[:], ALU.mult)

                        def q_nope(j):
                            pq = bank()
                            for kk in range(2):
                                sc.mm(pq[:], Wuq[:, kk, j * 128:(j + 1) * 128], cqn[:, kk, :],
                                      start=(kk == 0), stop=(kk == 1))
                            sc.act(qn[:, j, :], pq[:], AF.Copy)

                        def q_rope(r):
                            pr = bank()
                            pw = bank()
                            for kk in range(2):
                                sc.mm(pr[:], Wuq[:, kk, 512 + r * 128:512 + (r + 1) * 128], cqn[:, kk, :],
                                      start=(kk == 0), stop=(kk == 1))
                            for kk in range(2):
                                sc.mm(pw[:], Wuq[:, kk, 768 + r * 128:768 + (r + 1) * 128], cqn[:, kk, :],
                                      start=(kk == 0), stop=(kk == 1))
                            ta = tq[(2 * r) % 4]
                            tb = tq[(2 * r + 1) % 4]
                            sc.tt(ta[:], pr[:], rp[:, 0, :], ALU.mult)
                            sc.tt(tb[:], pw[:], rp[:, 1, :], ALU.mult)
                            sc.tt(qr[:, r, :], ta[:], tb[:], ALU.add, eng="pool")

                        glu_chunk(0)
                        glu_chunk(1)
                        q_nope(0)
                        q_rope(0)
                        glu_chunk(2)
                        q_nope(1)
                        q_rope(1)
                        glu_chunk(3)
                        q_nope(2)
                        q_nope(3)
                        for b_ in range(2):
                            sc.dma(qT[s].rearrange("(a b) r s -> b r a s", b=2)[b_][0:64, :, t0:t0 + T],
                                   qn[b_ * 64:(b_ + 1) * 64, :, :], writes=[("qT%d" % s, 0, 1, 8 * i + b_, 8 * i + b_ + 1)])
                        for m_ in range(4):
                            sc.dma(qT[s].rearrange("(r m) q s -> m q r s", m=4)[m_][64:96, :, t0:t0 + T],
                                   qr[m_ * 32:(m_ + 1) * 32, :, :],
                                   writes=[("qT%d" % s, 0, 1, 8 * i + 2 + m_, 8 * i + 3 + m_)])
                        sc.dma(gluS[s].rearrange("(c p) s -> p c s", p=128)[:, :, PAD + t0:PAD + t0 + T], g[:],
                               writes=[("gluS%d" % s, 0, 1, i, i + 1)])
                        if nflat < len(tiles):
                            p1_front2(nflat)
                        ko = knb[0]
                        for j in range(4):
                            ps = bank()
                            sc.mm(ps[:], Wkv[:, 0, j * 128:(j + 1) * 128], cqn[:, 2, :])
                            sc.act(ko[:, j, :], ps[:], AF.Copy)
                        sc.dma(knT[s].rearrange("(a b) r s -> (b r) a s", b=2)[:, :, t0:t0 + T], ko[:],
                               writes=[("knT%d" % s, 0, 1, i, i + 1)])
                        vo = vb[0]
                        for k in range(4):
                            ps = bank()
                            sc.mm(ps[:], cqn[:, 2, k * 128:(k + 1) * 128], Wkv[:, 0, 512:1024])
                            sc.act(vo[:, k, :, 0:64], ps[:].rearrange("p (h d) -> p h d", h=NH), AF.Copy)
                        sc.dma(vS[s][t0:t0 + T, :].rearrange("(k p) d -> p k d", p=128),
                               vo[:].rearrange("p k h d -> p k (h d)"),
                               writes=[("vS%d" % s, 0, 1, i, i + 1)])
                sc.barrier()
                sc.emit()

            with ExitStack() as es:
                SM = max(seqs)
                Kb = [sb(es, "Kb%d" % i, [128, SM], BF16) for i in range(2)]
                Vall = sb(es, "Vall", [128, SM // 128, NH * 65], BF16)
                Qb = [sb(es, "Qb%d" % i, [128, T], BF16) for i in range(3)]
                Pt = [sb(es, "Pt%d" % i, [128, 2 * T], BF16) for i in range(3)]
                oaug = [sb(es, "oaug%d" % i, [128, T], F32) for i in range(2)]
                rden = [sb(es, "rden%d" % i, [128, T], F32) for i in range(2)]
                obf = [sb(es, "obf%d" % i, [128, T], BF16) for i in range(2)]
                sel = sb(es, "sel", [128, 64], F32)
                sc.memset(sel[:], 0.0)
                sc.memset(sel[64:65, :], 1.0)
                cwt = sb(es, "cwt", [128, 4 * CW], F32)
                gin = [sb(es, "gin%d" % i, [128, 4, T + 2 * PAD], BF16) for i in range(2)]
                ysb = sb(es, "ysb", [128, 4, T], F32)
                ybf = sb(es, "ybf", [128, 4, T], BF16)
                ysq = sb(es, "ysq", [128, 4, T], BF16)
                mean = sb(es, "mean", [128, T], F32)
                msq = sb(es, "msq", [128, T], F32)
                var = sb(es, "var", [128, T], F32)
                rt3 = sb(es, "rt3", [128, T], F32)
                rs3 = sb(es, "rs3", [128, T], F32)
                tn = [sb(es, "tn%d" % i, [128, T], F32) for i in range(4)]
                sbf = [sb(es, "sbf%d" % i, [128, 4, T], BF16) for i in range(2)]
                sc.dma(cwt[:], cw_in[l])
                cstate = {"bank": None}

                def conv_gen():
                    ctiles = [(s_, j_) for s_ in range(NS) for j_ in range(seqs[s_] // T)]

                    def cload(n):
                        s_, j_ = ctiles[n]
                        sc.dma(gin[n % 2][:],
                               gluS[s_].rearrange("(c p) s -> p c s", p=128)[:, :, j_ * T:(j_ + 1) * T + 2 * PAD])

                    cload(0)
                    for n, (s_, j_) in enumerate(ctiles):
                        if n + 1 < len(ctiles):
                            cload(n + 1)
                        g = gin[n % 2]
                        for cp in range(2):
                            for tap in range(CW):
                                for c in (2 * cp, 2 * cp + 1):
                                    w = cwt[:, c * CW + tap:c * CW + tap + 1]
                                    if tap == 0:
                                        sc.ts(ysb[:, c, :], g[:, c, 0:T], w, cc("conv_b", l, c), ALU.mult, ALU.add)
                                    else:
                                        sc.stt(ysb[:, c, :], g[:, c, tap:tap + T], w, ysb[:, c, :], ALU.mult, ALU.add)
                                    yield
                            for c in (2 * cp, 2 * cp + 1):
                                sc.tt(ysq[:, c, :], ysb[:, c, :], ysb[:, c, :], ALU.mult)
                                yield
                                sc.copy(ybf[:, c, :], ysb[:, c, :])
                                yield
                        while cstate["bank"] is None:
                            yield
                        bk = cstate["bank"]
                        for c in range(4):
                            sc.mm(bk[:], ones_bf[:], ybf[:, c, :], start=(c == 0), stop=(c == 3))
                        sc.ts(mean[:], bk[:], 1.0 / CCH, None, ALU.mult)
                        for c in range(4):
                            sc.mm(bk[:], ones_bf[:], ysq[:, c, :], start=(c == 0), stop=(c == 3))
                        sc.tt(msq[:], mean[:], mean[:], ALU.mult)
                        sc.stt(var[:], bk[:], 1.0 / CCH, msq[:], ALU.mult, ALU.subtract)
                        sc.ts(var[:], var[:], 0.0, None, ALU.max)
                        for _ in range(8):
                            yield
                        sc.act(rt3[:], var[:], AF.Sqrt, bias=epsc[:, 0:1])
                        for _ in range(4):
                            yield
                        sc.recip(rs3[:], rt3[:])
                        yield
                        so = sbf[n % 2]
                        for c in range(4):
                            a_ = tn[c]
                            sc.tt(a_[:], ysb[:, c, :], mean[:], ALU.subtract)
                            yield
                            sc.tt(a_[:], a_[:], rs3[:], ALU.mult)
                            yield
                        for _ in range(8):
                            yield
                        for c in range(4):
                            sc.act(so[:, c, :], tn[c][:], AF.Silu, bias=cc("ln_b", l, c), scale=cc("ln_g", l, c))
                        sc.dma(sT[s_].rearrange("(c p) s -> p c s", p=128)[:, :, j_ * T:(j_ + 1) * T], so[:],
                               writes=[("sT%d" % s_, 0, 1, j_, j_ + 1)])
                        yield

                sgroups = [BV(0, 1024), BV(1024, 1024), BV(2048, 1024)]
                obanks = banks[6:8]
                dbanks = banks[6:8]
                nsc = 0
                iters = [(s, h, qi) for s in range(NS) for h in range(NH) for qi in range(seqs[s] // T)]

                def p2_loads(n):
                    s, h, qi = iters[n]
                    S = seqs[s]
                    NK = S // 128
                    if qi == 0:
                        if h < 2:
                            sc.dma(Kb[h % 2][64:96, 0:S], krT[s][:, :])
                        sc.dma(Kb[h % 2][0:64, 0:S], knT[s][h])
                    sc.dma(Qb[n % 3][0:96, :], qT[s][h][:, qi * T:(qi + 1) * T])

                pending = []

                def fin_b(it, s, h, qi):
                    oa = oaug[it % 2]
                    db = dbanks[it % 2]
                    sc.mm(db[0:64, :], sel[0:65, :], oa[0:65, :])
                    rd = rden[it % 2]
                    sc.recip(rd[0:64, :], db[0:64, :])
                    o_ = obf[it % 2]
                    sc.tt(o_[0:64, :], oa[0:64, :], rd[0:64, :], ALU.mult)
                    sc.dma(oT[s][h * 64:(h + 1) * 64, qi * T:(qi + 1) * T], o_[0:64, :],
                           writes=[("oT%d" % s, 0, 1, it, it + 1)])

                cg = conv_gen()

                def conv_pull(k=1):
                    for _ in range(k):
                        try:
                            next(cg)
                        except StopIteration:
                            break

                p2_loads(0)
                for it, (s, h, qi) in enumerate(iters):
                    S = seqs[s]
                    NK = S // 128
                    if h == 0 and qi == 0:
                        for k0 in range(0, NK, 8):
                            k1 = min(NK, k0 + 8)
                            sc.dma(Vall[:, k0:k1, :],
                                   vS[s][k0 * 128:k1 * 128, :].rearrange("(k p) d -> p k d", p=128))
                    if it + 1 < len(iters):
                        p2_loads(it + 1)
                    K_ = Kb[h % 2]
                    Q_ = Qb[it % 3]
                    ob = obanks[it % 2]
                    scs = {}
                    NG = NK // 2

                    def qk(g):
                        nonlocal nsc
                        grp = sgroups[nsc % 3]
                        nsc += 1
                        scs[g] = grp
                        for j in range(2):
                            kt = 2 * g + j
                            sc.mm(grp[:, j * T:(j + 1) * T], K_[0:96, kt * 128:(kt + 1) * 128], Q_[0:96, :])

                    qk(0)
                    if NG > 1:
                        qk(1)
                    for g in range(NG):
                        if g + 2 < NG:
                            qk(g + 2)
                        if g == 1 and pending:
                            fin_b(*pending.pop())
                        pt = Pt[g % 3]
                        sc.act(pt[:], scs.pop(g)[:], AF.Exp)
                        for j in range(2):
                            kt = 2 * g + j
                            sc.mm(ob[0:65, :], Vall[:, kt, h * 65:(h + 1) * 65], pt[:, j * T:(j + 1) * T],
                                  start=(kt == 0), stop=(kt == NK - 1))
                        cstate["bank"] = obanks[(it + 1) % 2] if (g >= 2 and not pending) else None
                        conv_pull(1)
                    cstate["bank"] = None
                    if pending:
                        fin_b(*pending.pop())
                    sc.copy(oaug[it % 2][0:65, :], ob[0:65, :])
                    pending.append((it, s, h, qi))
                if pending:
                    fin_b(*pending.pop())
                cstate["bank"] = obanks[len(iters) % 2]
                for _ in cg:
                    pass
                sc.barrier()
                sc.emit()

            with ExitStack() as es:
                Wg = sb(es, "Wg", [128, 8, 2048], BF16)
                Woa = sb(es, "Woa", [128, 4, D], BF16)
                Wpc = sb(es, "Wpc", [128, 4, D], BF16)
                Wo = sb(es, "Wo", [128, 8, D], BF16)
                xs = [sb(es, "xs%d" % i, [128, 8, T], F32) for i in range(2)]
                ss_ = [sb(es, "ss%d" % i, [128, 4, T], BF16) for i in range(2)]
                os_ = [sb(es, "os%d" % i, [128, 4, T], BF16) for i in range(2)]
                sq = sb(es, "sq", [128, 8, T], BF16)
                us = [sb(es, "u%d" % i, [128, 8, T], BF16) for i in range(2)]
                rt = sb(es, "rt", [128, T], F32)
                rs = sb(es, "rs", [128, T], F32)
                gA = [sb(es, "gA%d" % i, [128, T], F32) for i in range(2)]
                gC = [sb(es, "gC%d" % i, [128, T], F32) for i in range(2)]
                m1 = [sb(es, "m1%d" % i, [128, T], F32) for i in range(2)]
                m2 = [sb(es, "m2%d" % i, [128, T], F32) for i in range(2)]
                mg = sb(es, "mg", [128, 8, T], BF16)
                load_w(Wg, w_g[l], D, 2048, "g_mix", l)
                load_w(Woa, w_oa[l], 512, D)
                load_w(Wpc, w_pc[l], 512, D)
                load_w(Wo, w_out[l], D, D)
                tiles = [(s_, i_) for s_ in range(NS) for i_ in range(seqs[s_] // T)]

                def p3_loads(n):
                    s_, i_ = tiles[n]
                    sc.dma(xs[n % 2][:], xview(s_, i_ * T, T), reads=[("xT%d" % s_, 0, 1, i_, i_ + 1)])
                    sc.dma(ss_[n % 2][:], sT[s_].rearrange("(c p) s -> p c s", p=128)[:, :, i_ * T:(i_ + 1) * T])
                    sc.dma(os_[n % 2][:], oT[s_].rearrange("(c p) s -> p c s", p=128)[:, :, i_ * T:(i_ + 1) * T])

                def p3_front(n):
                    x = xs[n % 2]
                    sc.act(sq[:], x[:], AF.Square)
                    r = rstd_from((rt, rs), [sq[:, c, :] for c in range(8)], D, T)
                    scale_chunks(us[n % 2], x, r, T)

                p3_loads(0)
                p3_front(0)
                for n, (s, i) in enumerate(tiles):
                    if True:
                        t0 = i * T
                        x = xs[n % 2]
                        s_in = ss_[n % 2]
                        o_in = os_[n % 2]
                        u = us[n % 2]
                        if n + 1 < len(tiles):
                            p3_loads(n + 1)
                        for c in range(8):
                            if c == 5 and n + 1 < len(tiles):
                                p3_front(n + 1)
                            pA = bank()
                            pC = bank()
                            pa = bank()
                            pc_ = bank()
                            for k in range(8):
                                sc.mm(pA[:], Wg[:, k, c * 128:(c + 1) * 128], u[:, k, :], start=(k == 0), stop=(k == 7))
                            for k in range(8):
                                sc.mm(pC[:], Wg[:, k, D + c * 128:D + (c + 1) * 128], u[:, k, :],
                                      start=(k == 0), stop=(k == 7))
                            for k in range(4):
                                sc.mm(pa[:], Woa[:, k, c * 128:(c + 1) * 128], o_in[:, k, :], start=(k == 0), stop=(k == 3))
                            for k in range(4):
                                sc.mm(pc_[:], Wpc[:, k, c * 128:(c + 1) * 128], s_in[:, k, :], start=(k == 0), stop=(k == 3))
                            a = gA[c % 2]
                            b = gC[c % 2]
                            sc.act(a[:], pA[:], AF.Sigmoid, bias=cc("b_gate", l, c))
                            sc.act(b[:], pC[:], AF.Sigmoid, bias=cc("b_gate", l, 8 + c))
                            sc.tt(m1[c % 2][:], pa[:], a[:], ALU.mult)
                            sc.tt(m2[c % 2][:], pc_[:], b[:], ALU.mult)
                            sc.tt(mg[:, c, :], m1[c % 2][:], m2[c % 2][:], ALU.add, eng=("pool" if c % 2 == 1 else "dve"))
                        for c in range(8):
                            ps = bank()
                            for k in range(8):
                                sc.mm(ps[:], Wo[:, k, c * 128:(c + 1) * 128], mg[:, k, :], start=(k == 0), stop=(k == 7))
                            sc.tt(x[:, c, :], ps[:], x[:, c, :], ALU.add)
                        sc.dma(xview(s, t0, T), x[:], writes=[("xT%d" % s, 0, 1, i, i + 1)])
                sc.barrier()
                sc.emit()

            with ExitStack() as es:
                T4 = 256
                Wu = sb(es, "Wu", [128, 8, DFF], BF16)
                Wd = sb(es, "Wd", [128, 32, D], BF16)
                xs = [sb(es, "xs%d" % i, [128, 8, T4], F32) for i in range(2)]
                sq = sb(es, "sq", [128, 8, T4], BF16)
                xns = [sb(es, "xn%d" % i, [128, 8, T4], BF16) for i in range(2)]
                rt = sb(es, "rt", [128, T4], F32)
                rs = sb(es, "rs", [128, T4], F32)
                rl = [sb(es, "rl%d" % i, [128, T4], F32) for i in range(3)]
                hb = sb(es, "hb", [128, 32, T4], BF16)
                load_w(Wu, w_up[l], D, DFF, "g_mlp", l)
                load_w(Wd, w_down[l], DFF, D)
                tiles = [(s_, i_) for s_ in range(NS) for i_ in range(seqs[s_] // T4)]

                def p4_loads(n):
                    s_, i_ = tiles[n]
                    sc.dma(xs[n % 2][:], xview(s_, i_ * T4, T4), reads=[("xT%d" % s_, 0, 1, i_ * T4, i_ * T4 + T4)])

                def p4_front(n):
                    x = xs[n % 2]
                    sc.act(sq[:], x[:], AF.Square)
                    r = rstd_from((rt, rs), [sq[:, c, :] for c in range(8)], D, T4)
                    scale_chunks(xns[n % 2], x, r, T4)

                p4_loads(0)
                p4_front(0)
                for n, (s, i) in enumerate(tiles):
                    if True:
                        t0 = i * T4
                        x = xs[n % 2]
                        xn = xns[n % 2]
                        if n + 1 < len(tiles):
                            p4_loads(n + 1)
                        for f in range(32):
                            if f == 20 and n + 1 < len(tiles):
                                p4_front(n + 1)
                            ps = bank()
                            for k in range(8):
                                sc.mm(ps[:, 0:T4], Wu[:, k, f * 128:(f + 1) * 128], xn[:, k, :],
                                      start=(k == 0), stop=(k == 7))
                            rr = rl[f % 3]
                            sc.act(rr[:], ps[:, 0:T4], AF.Relu)
                            sc.tt(hb[:, f, :], rr[:], rr[:], ALU.mult)
                        for c in range(8):
                            ps = bank()
                            for f in range(32):
                                sc.mm(ps[:, 0:T4], Wd[:, f, c * 128:(c + 1) * 128], hb[:, f, :],
                                      start=(f == 0), stop=(f == 31))
                            sc.tt(x[:, c, :], ps[:, 0:T4], x[:, c, :], ALU.add)
                        sc.dma(xview(s, t0, T4), x[:], writes=[("xT%d" % s, 0, 1, t0, t0 + T4)])
                sc.barrier()
                sc.emit()

            with ExitStack() as es:
                last = (l == DEPTH - 1)
                Wpg = sb(es, "Wpg", [128, 8, D], BF16)
                Wpl = sb(es, "Wpl", [128, 2, D], BF16)
                xs = [sb(es, "xs%d" % i, [128, 8, T], F32) for i in range(2)]
                ptok = [sb(es, "ptok%d" % i, [128, 4, PLE], F32) for i in range(2)]
                pTs = [sb(es, "pT%d" % i, [128, 2, T], BF16) for i in range(2)]
                xbs = [sb(es, "xb%d" % i, [128, 8, T], BF16) for i in range(2)]
                E = sb(es, "E", [128, 8, T], F32)
                esq = sb(es, "esq", [128, 8, T], BF16)
                rt = sb(es, "rt", [128, T], F32)
                rs = sb(es, "rs", [128, T], F32)
                rt2 = sb(es, "rt2", [128, T], F32)
                rs2 = sb(es, "rs2", [128, T], F32)
                sg = [sb(es, "sg%d" % i, [128, T], F32) for i in range(2)]
                en = [sb(es, "en%d" % i, [128, T], F32) for i in range(2)]
                yt = sb(es, "yt", [128, 8, T], F32) if last else None
                fsq = sb(es, "fsq", [128, 8, T], BF16) if last else None
                ytok = [sb(es, "ytok%d" % i, [128, D], F32) for i in range(2)] if last else None
                load_w(Wpg, w_pg[l], D, D)
                load_w(Wpl, w_ple[l], PLE, D)
                tiles = [(s_, i_) for s_ in range(NS) for i_ in range(seqs[s_] // T)]

                def p5_loads(n):
                    s_, i_ = tiles[n]
                    sc.dma(xs[n % 2][:], xview(s_, i_ * T, T), reads=[("xT%d" % s_, 0, 1, i_, i_ + 1)])
                    sc.dma(ptok[n % 2][:], p_in[s_][l][i_ * T:(i_ + 1) * T, :].rearrange("(s p) d -> p s d", p=128))

                def p5_front(n):
                    x = xs[n % 2]
                    pk = ptok[n % 2]
                    for j in range(2):
                        ps = bank()
                        for k in range(4):
                            sc.tr(ps[:, k * 128:(k + 1) * 128], pk[:, k, j * 128:(j + 1) * 128], ident[:])
                        sc.copy(pTs[n % 2][:, j, :], ps[:])
                    sc.act(xbs[n % 2][:, 0:4, :], x[:, 0:4, :], AF.Copy)
                    sc.copy(xbs[n % 2][:, 4:8, :], x[:, 4:8, :])

                fin_q = []
                fin_r = []
                ycnt = 0

                def final_a2():
                    n_, s_, i_ = fin_q[0]
                    x_ = xs[n_ % 2]
                    rf = rstd_from((rt2, rs2), [fsq[:, c, :] for c in range(8)], D, T)
                    for c in range(8):
                        sc.stt(yt[:, c, :], x_[:, c, :], cc("g_final", 0, c), rf[:], ALU.mult, ALU.mult)

                def final_b():
                    nonlocal ycnt
                    n_, s_, i_ = fin_q.pop(0)
                    for k in range(4):
                        yk = ytok[k % 2]
                        for half in range(2):
                            ps = bank()
                            for c4 in range(4):
                                c = half * 4 + c4
                                sc.tr(ps[:, c4 * 128:(c4 + 1) * 128], yt[:, c, k * 128:(k + 1) * 128], ident[:])
                            if half == 0:
                                sc.copy(yk[:, 0:512], ps[:])
                            else:
                                sc.act(yk[:, 512:1024], ps[:], AF.Copy)
                        ycnt += 1
                        sc.dma(y_out[s_][i_ * T + k * 128:i_ * T + (k + 1) * 128, :], yk[:],
                               writes=[("y%d" % s_, 0, 1, ycnt, ycnt + 1)])

                p5_loads(0)
                p5_front(0)
                for n, (s, i) in enumerate(tiles):
                    if True:
                        t0 = i * T
                        x = xs[n % 2]
                        pk = ptok[n % 2]
                        pT = pTs[n % 2]
                        xb = xbs[n % 2]
                        for c in range(8):
                            ps = bank()
                            for j in range(2):
                                sc.mm(ps[:], Wpl[:, j, c * 128:(c + 1) * 128], pT[:, j, :], start=(j == 0), stop=(j == 1))
                            sc.act(E[:, c, :], ps[:], AF.Copy)
                            sc.act(esq[:, c, :], ps[:], AF.Square)
                        if last and fin_q:
                            final_a2()
                        if n + 1 < len(tiles):
                            p5_loads(n + 1)
                        r = rstd_from((rt, rs), [esq[:, c, :] for c in range(8)], D, T)
                        for c in range(8):
                            if c == 4 and n + 1 < len(tiles):
                                p5_front(n + 1)
                            if c == 2 and last and fin_q:
                                final_b()
                            ps = bank()
                            for k in range(8):
                                sc.mm(ps[:], Wpg[:, k, c * 128:(c + 1) * 128], xb[:, k, :], start=(k == 0), stop=(k == 7))
                            g_ = sg[c % 2]
                            e_ = en[c % 2]
                            sc.act(g_[:], ps[:], AF.Sigmoid)
                            sc.stt(e_[:], E[:, c, :], cc("g_ple", l, c), r[:], ALU.mult, ALU.mult)
                            sc.tt(e_[:], e_[:], g_[:], ALU.mult)
                            sc.tt(x[:, c, :], x[:, c, :], e_[:], ALU.add, eng="pool")
                        if not last:
                            sc.dma(xview(s, t0, T), x[:], writes=[("xT%d" % s, 0, 1, i, i + 1)])
                        else:
                            sc.act(fsq[:], x[:], AF.Square)
                            fin_q.append((n, s, i))
                if last:
                    while fin_q:
                        final_a2()
                        final_b()
                sc.barrier()
                sc.emit()
    return nc


def _rope_tables(smax):
    inv = (1.0 / (np.float32(10000.0) ** (np.arange(0, ROPE, 2, dtype=np.float32) / np.float32(ROPE)))).astype(np.float32)
    ang = (np.arange(smax, dtype=np.float32)[:, None] * inv[None, :]).astype(np.float32)
    cos = np.cos(ang).astype(np.float32).T
    sin = np.sin(ang).astype(np.float32).T
    tab = np.zeros((32, 2, smax), np.float32)
    tab[0:16, 0] = cos
    tab[16:32, 0] = cos
    tab[0:16, 1] = -sin
    tab[16:32, 1] = sin
    return np.ascontiguousarray(np.tile(tab, (4, 1, 1)))


def _prep_shared(inp, smax):
    f = lambda a: np.ascontiguousarray(np.asarray(a, dtype=np.float32))
    w_in = f(inp["w_in"])
    krsw = np.concatenate([w_in[:, :, 400:416], w_in[:, :, 384:400]], axis=2)
    w_p1 = np.concatenate([w_in[:, :, 0:1440], krsw], axis=2)
    w_g = w_in[:, :, 1440:3488]
    wuq = f(inp["w_uq"])
    nop, rop, sw = [], [], []
    for h in range(NH):
        b = h * 96 + 64
        nop.append(wuq[:, :, h * 96:h * 96 + 64])
        rop.append(wuq[:, :, b:b + 32])
        sw.append(wuq[:, :, b + 16:b + 32])
        sw.append(wuq[:, :, b:b + 16])
    w_uq = np.concatenate(nop + rop + sw, axis=2)
    wkv = f(inp["w_ukv"]).reshape(DEPTH, KVL, NH, 128)
    w_kv = np.concatenate([wkv[:, :, :, 0:64].reshape(DEPTH, KVL, 512),
                           wkv[:, :, :, 64:128].reshape(DEPTH, KVL, 512)], axis=2)
    cwv = f(inp["conv_w"])
    cw = cwv.reshape(DEPTH, CW, 4, 128).transpose(0, 3, 2, 1).reshape(DEPTH, 128, 4 * CW)
    cst = np.zeros((128, NCST), np.float32)
    for (name, l), c0 in CST.items():
        v = f(inp[name])
        v = v if name == "g_final" else v[l]
        k = v.shape[0] // 128
        cst[:, c0:c0 + k] = v.reshape(k, 128).T
    return {
        "w_p1": f(w_p1), "w_g": f(w_g), "w_uq": f(w_uq), "w_kv": f(w_kv),
        "w_oa": f(inp["w_oa"]), "w_pc": f(inp["w_pc"]), "w_out": f(inp["w_out"]),
        "w_up": f(inp["w_up"]), "w_down": f(inp["w_down"]), "w_pg": f(inp["w_ple_gate"]),
        "w_ple": f(inp["w_ple"]), "cw": f(cw), "cst": cst,
        "ident": np.eye(128, dtype=np.float32), "rope": _rope_tables(smax),
    }


def run_cores(inp, n_cores=8):
    xp = np.asarray(inp["x_prompt"], np.float32)
    xsm = np.asarray(inp["x_sample"], np.float32)
    pp = np.asarray(inp["p_prompt"], np.float32)
    psm = np.asarray(inp["p_sample"], np.float32)
    S0, S1 = xp.shape[1], xsm.shape[1]
    nb0, nb1 = xp.shape[0], xsm.shape[0]
    nc = build_program([S0, S1])
    shared = _prep_shared(inp, max(S0, S1))
    in_maps = []
    for c in range(n_cores):
        m = dict(shared)
        b0 = c % nb0
        b1 = c % nb1
        m["x0"] = np.ascontiguousarray(xp[b0])
        m["x1"] = np.ascontiguousarray(xsm[b1])
        m["p0"] = np.ascontiguousarray(pp[:, b0])
        m["p1"] = np.ascontiguousarray(psm[:, b1])
        in_maps.append(m)
    res = run_bass_kernel_spmd(nc, in_maps, core_ids=list(range(n_cores)))
    yp = np.stack([np.asarray(res.results[b % n_cores]["y0"], np.float32) for b in range(nb0)], axis=0)
    ys = np.stack([np.asarray(res.results[b % n_cores]["y1"], np.float32) for b in range(nb1)], axis=0)
    return yp, ys


def kernel(**inputs):
    yp, ys = run_cores(inputs, 8)
    return (yp, ys)
```

```python
import numpy as np
from contextlib import ExitStack
import concourse.bass as bass
import concourse.mybir as mybir
from concourse.bass_utils import run_bass_kernel_spmd

F32 = mybir.dt.float32
BF16 = mybir.dt.bfloat16
AF = mybir.ActivationFunctionType
ALU = mybir.AluOpType

D = 1024
DEPTH = 2
NH = 8
QL = 256
KVL = 128
NOPE = 64
ROPE = 32
VH = 64
CCH = 512
CW = 31
PAD = 15
DFF = 4096
PLE = 256
EPS = 1e-6
ATTN_SCALE = float((NOPE + ROPE) ** -0.5)
T = 512
NCST = 128


class _Op:
    __slots__ = ("stream", "fn", "deps", "dma", "needs_inc", "incval", "dsem", "dval", "waits")

    def __init__(self, stream, fn, dma):
        self.stream = stream
        self.fn = fn
        self.deps = set()
        self.dma = dma
        self.needs_inc = False
        self.incval = None
        self.dsem = None
        self.dval = None
        self.waits = None


def _region(ap):
    t = ap.tensor
    shp = list(t.shape)
    fsz = 1
    for d in shp[1:]:
        fsz *= int(d)
    off = int(ap.offset)
    dims = [(int(s), int(c)) for s, c in ap.ap]
    p0 = off // fsz
    f0 = off % fsz
    if dims and dims[0][0] == fsz:
        pc = dims[0][1]
        rest = dims[1:]
    else:
        pc = 1
        rest = dims
    span = 1
    for s, c in rest:
        span += (c - 1) * abs(s)
    return (t.name, p0, p0 + pc, f0, f0 + span)


class Sched:
    STREAMS = ("pe", "act", "dve", "pool", "sp")
    KDMA = 8

    def __init__(self, nc, es):
        self.nc = nc
        self.ops = []
        self.track = {}
        self.sem = {s: es.enter_context(nc.semaphore("s_" + s)) for s in ("pe", "act", "dve", "pool")}
        self.dsem = {s: [es.enter_context(nc.semaphore("d_%s%d" % (s, j))) for j in range(self.KDMA)]
                     for s in ("sp", "pool")}
        self.cnt = {s: 0 for s in self.sem}
        self.dcnt = {s: [0] * self.KDMA for s in self.dsem}
        self.dnum = {s: 0 for s in self.dsem}
        self.waited = {s: {} for s in self.STREAMS}

    def add(self, stream, fn, reads=(), writes=(), dma=False):
        op = _Op(stream, fn, dma)
        idx = len(self.ops)
        self.ops.append(op)
        for regs, w in ((reads, False), (writes, True)):
            for r in regs:
                if r is None:
                    continue
                if not isinstance(r, tuple):
                    r = _region(r)
                self._access(idx, op, r, w)
        for d in list(op.deps):
            dop = self.ops[d]
            if dop.stream == stream and stream == "pe":
                op.deps.discard(d)
                continue
            if not dop.dma:
                dop.needs_inc = True
        return idx

    def _access(self, idx, op, r, w):
        key, plo, phi, lo, hi = r
        lst = self.track.get(key)
        if lst is None:
            lst = []
            self.track[key] = lst
        keep = []
        for e in lst:
            eplo, ephi, elo, ehi, eop, ew, estream, edma = e
            ov = eplo < phi and plo < ephi and elo < hi and lo < ehi
            if ov and (w or ew) and eop != idx:
                op.deps.add(eop)
            contained = ov and plo <= eplo and ephi <= phi and lo <= elo and ehi <= hi
            if contained and w:
                continue
            if contained and (not w) and (not ew) and estream == op.stream and not edma and not op.dma:
                continue
            keep.append(e)
        keep.append((plo, phi, lo, hi, idx, w, op.stream, op.dma))
        self.track[key] = keep

    def barrier(self):
        last = {}
        dmas = []
        for i, op in enumerate(self.ops):
            if op.fn is None:
                continue
            if op.dma:
                dmas.append(i)
            else:
                last[op.stream] = i
        for s in self.STREAMS:
            op = _Op(s, None, False)
            for st, i in last.items():
                if st != s:
                    op.deps.add(i)
                    self.ops[i].needs_inc = True
            for i in dmas:
                op.deps.add(i)
            self.ops.append(op)
        self.track = {}

    def emit(self):
        nc = self.nc
        ops = self.ops
        for op in ops:
            if op.fn is None:
                continue
            if op.dma:
                s = op.stream
                j = self.dnum[s] % self.KDMA
                self.dnum[s] += 1
                self.dcnt[s][j] += 16
                op.dsem = self.dsem[s][j]
                op.dval = self.dcnt[s][j]
            elif op.needs_inc:
                self.cnt[op.stream] += 1
                op.incval = self.cnt[op.stream]
        for op in ops:
            w = {}
            for d in op.deps:
                dop = ops[d]
                if dop.dma:
                    sem, val = dop.dsem, dop.dval
                else:
                    sem, val = self.sem[dop.stream], dop.incval
                k = id(sem)
                if k not in w or w[k][1] < val:
                    w[k] = (sem, val)
            if op.dma and op.dval > 16:
                k = id(op.dsem)
                v = op.dval - 16
                if k not in w or w[k][1] < v:
                    w[k] = (op.dsem, v)
            wd = self.waited[op.stream]
            out = []
            for k, (sem, val) in w.items():
                if wd.get(k, 0) >= val:
                    continue
                wd[k] = val
                out.append((sem, val))
            op.waits = out
        by = {s: [] for s in self.STREAMS}
        for op in ops:
            by[op.stream].append(op)
        sems = self.sem

        def run(stream, e):
            for op in by[stream]:
                for sem, val in op.waits:
                    e.wait_ge(sem, val)
                if op.fn is None:
                    continue
                ins = op.fn(e)
                if op.dma:
                    ins.then_inc(op.dsem, 16)
                elif op.needs_inc:
                    ins.then_inc(sems[stream], 1)

        with nc.Block() as block:
            @block.tensor
            def _(e):
                run("pe", e)

            @block.scalar
            def _(e):
                run("act", e)

            @block.vector
            def _(e):
                run("dve", e)

            @block.gpsimd
            def _(e):
                run("pool", e)

            @block.sync
            def _(e):
                run("sp", e)
        self.ops = []
        self.track = {}

    def mm(self, out, lhsT, rhs, start=True, stop=True):
        return self.add("pe", lambda e: e.matmul(out, lhsT=lhsT, rhs=rhs, start=start, stop=stop),
                        reads=[lhsT, rhs], writes=[out])

    def tr(self, out, in_, ident):
        return self.add("pe", lambda e: e.transpose(out, in_, ident), reads=[in_, ident], writes=[out])

    def act(self, out, in_, func, bias=None, scale=None):
        kw = {}
        rd = [in_]
        if bias is not None:
            kw["bias"] = bias
            if not isinstance(bias, (int, float)):
                rd.append(bias)
        if scale is not None:
            kw["scale"] = scale
            if not isinstance(scale, (int, float)):
                rd.append(scale)
        return self.add("act", lambda e: e.activation(out, in_, func, **kw), reads=rd, writes=[out])

    def tt(self, out, in0, in1, op, eng="dve"):
        return self.add(eng, lambda e: e.tensor_tensor(out, in0, in1, op), reads=[in0, in1], writes=[out])

    def ts(self, out, in0, s1, s2, op0, op1=None, eng="dve"):
        rd = [in0]
        for s in (s1, s2):
            if s is not None and not isinstance(s, (int, float)):
                rd.append(s)
        if op1 is None:
            return self.add(eng, lambda e: e.tensor_scalar(out, in0, s1, None, op0), reads=rd, writes=[out])
        return self.add(eng, lambda e: e.tensor_scalar(out, in0, s1, s2, op0, op1), reads=rd, writes=[out])

    def stt(self, out, in0, scalar, in1, op0, op1):
        rd = [in0, in1]
        if not isinstance(scalar, (int, float)):
            rd.append(scalar)
        return self.add("dve", lambda e: e.scalar_tensor_tensor(out, in0, scalar, in1, op0, op1),
                        reads=rd, writes=[out])

    def copy(self, out, in_, eng="dve"):
        return self.add(eng, lambda e: e.tensor_copy(out, in_), reads=[in_], writes=[out])

    def recip(self, out, in_):
        return self.add("dve", lambda e: e.reciprocal(out, in_), reads=[in_], writes=[out])

    def memset(self, ap, val, eng="dve"):
        return self.add(eng, lambda e: e.memset(ap, val), writes=[ap])

    def dma(self, out, in_, reads=(), writes=(), stream="sp"):
        rd = list(reads)
        wr = list(writes)
        if in_.tensor.__class__.__name__ != "DRamTensorHandle":
            rd.append(in_)
        if out.tensor.__class__.__name__ != "DRamTensorHandle":
            wr.append(out)
        return self.add(stream, lambda e: e.dma_start(out=out, in_=in_), reads=rd, writes=wr, dma=True)


def _cst_cols():
    cols = {}
    n = 0
    for l in range(DEPTH):
        for name, k in (("conv_b", 4), ("ln_g", 4), ("ln_b", 4), ("b_gate", 16), ("g_mix", 8),
                        ("g_mlp", 8), ("g_ple", 8), ("g_q", 2), ("g_kv", 1)):
            cols[(name, l)] = n
            n += k
    cols[("g_final", 0)] = n
    n += 8
    assert n <= NCST
    return cols


CST = _cst_cols()


def build_program(seqs):
    nc = bass.Bass("TRN2", target_bir_lowering=False)
    NS = len(seqs)
    SMAX = max(seqs)

    def din(name, shape, dt=F32):
        return nc.dram_tensor(name, list(shape), dt, kind="ExternalInput").ap()

    def dscr(name, shape, dt):
        return nc.dram_tensor(name, list(shape), dt, kind="Internal").ap()

    x_in = [din("x%d" % s, [seqs[s], D]) for s in range(NS)]
    p_in = [din("p%d" % s, [DEPTH, seqs[s], PLE]) for s in range(NS)]
    y_out = [nc.dram_tensor("y%d" % s, [seqs[s], D], F32, kind="ExternalOutput").ap() for s in range(NS)]
    w_p1 = din("w_p1", [DEPTH, D, 1472])
    w_g = din("w_g", [DEPTH, D, 2048])
    w_uq = din("w_uq", [DEPTH, QL, 1024])
    w_kv = din("w_kv", [DEPTH, KVL, 1024])
    w_oa = din("w_oa", [DEPTH, 512, D])
    w_pc = din("w_pc", [DEPTH, 512, D])
    w_out = din("w_out", [DEPTH, D, D])
    w_up = din("w_up", [DEPTH, D, DFF])
    w_down = din("w_down", [DEPTH, DFF, D])
    w_pg = din("w_pg", [DEPTH, D, D])
    w_ple = din("w_ple", [DEPTH, PLE, D])
    cw_in = din("cw", [DEPTH, 128, 4 * CW])
    cst_in = din("cst", [128, NCST])
    ident_in = din("ident", [128, 128])
    rope_in = din("rope", [128, 2, SMAX])

    xT = [dscr("xT%d" % s, [D, seqs[s]], F32) for s in range(NS)]
    qT = [dscr("qT%d" % s, [NH, 96, seqs[s]], BF16) for s in range(NS)]
    knT = [dscr("knT%d" % s, [NH, 64, seqs[s]], BF16) for s in range(NS)]
    krT = [dscr("krT%d" % s, [32, seqs[s]], BF16) for s in range(NS)]
    vS = [dscr("vS%d" % s, [seqs[s], NH * 65], BF16) for s in range(NS)]
    sT = [dscr("sT%d" % s, [512, seqs[s]], BF16) for s in range(NS)]
    xnS = [dscr("xnS%d" % s, [D, seqs[s]], BF16) for s in range(NS)]
    gluS = [dscr("gluS%d" % s, [512, seqs[s] + 2 * PAD], BF16) for s in range(NS)]
    oT = [dscr("oT%d" % s, [512, seqs[s]], BF16) for s in range(NS)]

    with ExitStack() as es0:
        sc = Sched(nc, es0)

        uniq = [0]

        def sb(es, name, shape, dt):
            uniq[0] += 1
            return es.enter_context(nc.sbuf_tensor("%s_%d" % (name, uniq[0]), list(shape), dt))

        psum_all = es0.enter_context(nc.psum_tensor("psum_all", [128, 8 * 512], F32))

        class BV:
            def __init__(self, base, width=512):
                self.base = base
                self.width = width

            def __getitem__(self, idx):
                if not isinstance(idx, tuple):
                    idx = (idx, slice(None))
                rows, cols = idx
                c0 = cols.start or 0
                c1 = self.width if cols.stop is None else cols.stop
                return psum_all[rows, self.base + c0:self.base + c1]

        banks = [BV(i * 512) for i in range(8)]
        ones_bf = sb(es0, "ones_bf", [128, 128], BF16)
        ident = sb(es0, "ident", [128, 128], F32)
        cst = sb(es0, "cst", [128, NCST], F32)
        stg = [sb(es0, "stg%d" % i, [128, 2048], F32) for i in range(2)]
        state = {"bank": 0, "stg": 0, "cv": 0}

        def bank():
            b = banks[state["bank"] % 8]
            state["bank"] += 1
            return b

        def cc(name, l, j=0):
            c = CST[(name, l)] + j
            return cst[:, c:c + 1]

        sc.memset(ones_bf[:], 1.0)
        sc.dma(ident[:], ident_in[:, :])
        sc.dma(cst[:], cst_in[:, :])
        sc.barrier()
        sc.emit()

        def load_w(dst, src, K, N, scale_name=None, l=0, extra=None):
            KC = (K + 127) // 128
            for kc in range(KC):
                rows = min(128, K - kc * 128)
                for n0 in range(0, N, 2048):
                    n1 = min(N, n0 + 2048)
                    st = stg[state["stg"] % 2]
                    state["stg"] += 1
                    sc.dma(st[0:rows, 0:n1 - n0], src[kc * 128:kc * 128 + rows, n0:n1])
                    o = dst[0:rows, kc, n0:n1]
                    i_ = st[0:rows, 0:n1 - n0]
                    use_act = (state["cv"] % 2 == 1)
                    state["cv"] += 1
                    if scale_name is None:
                        if use_act:
                            sc.act(o, i_, AF.Copy)
                        else:
                            sc.copy(o, i_)
                    else:
                        g = cc(scale_name, l, kc)[0:rows, :]
                        if extra is not None:
                            sc.ts(o, i_, g, extra, ALU.mult, ALU.mult)
                        elif use_act:
                            sc.act(o, i_, AF.Copy, scale=g)
                        else:
                            sc.ts(o, i_, g, None, ALU.mult)

        def rstd_from(es_bufs, sq_chunks, nfeat, Tn):
            rt, rs = es_bufs
            ps = bank()
            n = len(sq_chunks)
            for i, q in enumerate(sq_chunks):
                sc.mm(ps[:, 0:Tn], ones_bf[:], q, start=(i == 0), stop=(i == n - 1))
            sc.act(rt[:, 0:Tn], ps[:, 0:Tn], AF.Sqrt, bias=epsc[:, 0:1], scale=1.0 / nfeat)
            sc.recip(rs[:, 0:Tn], rt[:, 0:Tn])
            return rs

        def scale_chunks(dst, x, r, Tn, n=8):
            for c in range(n):
                eng = "dve"
                sc.tt(dst[:, c, :], x[:, c, :], r[:, 0:Tn], ALU.mult, eng=eng)

        epsc = sb(es0, "epsc", [128, 1], F32)
        sc.memset(epsc[:], EPS)

        def xview(s, t0, Tn):
            return xT[s].rearrange("(c p) s -> p c s", p=128)[:, :, t0:t0 + Tn]

        for l in range(DEPTH):
            with ExitStack() as es:
                Wa = sb(es, "Wa", [128, 8, 1472], BF16)
                Wuq = sb(es, "Wuq", [128, 2, 1024], BF16)
                Wkv = sb(es, "Wkv", [128, 1, 1024], BF16)
                xtok = sb(es, "xtok", [128, 4, D], F32) if l == 0 else None
                xs = [sb(es, "xs%d" % i, [128, 8, T], F32) for i in range(2)]
                ropes = [sb(es, "rope%d" % i, [128, 2, T], F32) for i in range(3)]
                sq = sb(es, "sq", [128, 8, T], BF16)
                us = [sb(es, "u%d" % i, [128, 8, T], BF16) for i in range(2)]
                rt = sb(es, "rt", [128, T], F32)
                rs = sb(es, "rs", [128, T], F32)
                rt2 = rt
                rs2 = sb(es, "rs2", [128, T], F32)
                cq = sb(es, "cq", [128, 3, T], F32)
                cqsq = sb(es, "cqsq", [128, 3, T], BF16)
                cqn = sb(es, "cqn", [128, 3, T], BF16)
                t1 = sb(es, "t1", [128, T], F32)
                t2 = sb(es, "t2", [128, T], F32)
                tq = [sb(es, "tq%d" % i, [128, T], F32) for i in range(4)]
                krb = [sb(es, "krb%d" % i, [128, T], BF16) for i in range(2)]
                qn = sb(es, "qn", [128, 4, T], BF16)
                qr = sb(es, "qr", [128, 2, T], BF16)
                knb = [sb(es, "knb%d" % i, [128, 4, T], BF16) for i in range(1)]
                vb = [sb(es, "vb%d" % i, [128, 4, NH, 65], BF16) for i in range(1)]
                sig = [sb(es, "sig%d" % i, [128, T], F32) for i in range(2)]
                glu = [sb(es, "glu%d" % i, [128, 4, T], BF16) for i in range(2)]
                zpad = sb(es, "zpad", [128, 4, PAD], BF16)
                rs3 = sb(es, "rs3", [128, T], F32)

                load_w(Wa, w_p1[l], D, 1472, "g_mix", l)
                load_w(Wuq, w_uq[l], QL, 1024, "g_q", l, extra=ATTN_SCALE)
                load_w(Wkv, w_kv[l], KVL, 1024, "g_kv", l)
                sc.memset(zpad[:], 0.0)
                sc.memset(vb[0][:, :, :, 64:65], 1.0)
                for s_ in range(NS):
                    gv = gluS[s_].rearrange("(c p) s -> p c s", p=128)
                    sc.dma(gv[:, :, 0:PAD], zpad[:])
                    sc.dma(gv[:, :, PAD + seqs[s_]:2 * PAD + seqs[s_]], zpad[:])
                tiles = [(s_, i_) for s_ in range(NS) for i_ in range(seqs[s_] // T)]

                def p1_loads(n):
                    s_, i_ = tiles[n]
                    sc.dma(ropes[n % 3][:], rope_in[:, :, i_ * T:(i_ + 1) * T])
                    if l == 0:
                        sc.dma(xtok[:], x_in[s_][i_ * T:(i_ + 1) * T, :].rearrange("(s p) d -> p s d", p=128))
                    else:
                        sc.dma(xs[n % 2][:], xview(s_, i_ * T, T), reads=[("xT%d" % s_, 0, 1, i_, i_ + 1)])

                def p1_front(n):
                    s_, i_ = tiles[n]
                    x = xs[n % 2]
                    if l == 0:
                        for c in range(8):
                            ps = bank()
                            for k in range(4):
                                sc.tr(ps[:, k * 128:(k + 1) * 128], xtok[:, k, c * 128:(c + 1) * 128], ident[:])
                            if c % 2 == 0:
                                sc.copy(x[:, c, :], ps[:])
                            else:
                                sc.act(x[:, c, :], ps[:], AF.Copy)
                        if n + 1 < len(tiles):
                            p1_loads(n + 1)
                        sc.dma(xview(s_, i_ * T, T), x[:], writes=[("xT%d" % s_, 0, 1, i_, i_ + 1)])
                    else:
                        if n + 1 < len(tiles):
                            p1_loads(n + 1)
                    sc.act(sq[:], x[:], AF.Square)

                def p1_front2(n):
                    x = xs[n % 2]
                    r = rstd_from((rt, rs), [sq[:, c, :] for c in range(8)], D, T)
                    scale_chunks(us[n % 2], x, r, T)

                p1_loads(0)
                p1_front(0)
                p1_front2(0)
                nflat = 0
                for s in range(NS):
                    S = seqs[s]
                    NT = S // T

                    for i in range(NT):
                        t0 = i * T
                        x = xs[nflat % 2]
                        rp = ropes[nflat % 3]
                        u = us[nflat % 2]
                        nflat += 1

                        def zmm(ps_ap, col0, ncol):
                            for k in range(8):
                                sc.mm(ps_ap, Wa[:, k, col0:col0 + ncol], u[:, k, :], start=(k == 0), stop=(k == 7))

                        for j in range(3):
                            ps = bank()
                            zmm(ps[:], j * 128, 128)
                            sc.act(cq[:, j, :], ps[:], AF.Copy)
                            sc.act(cqsq[:, j, :], ps[:], AF.Square)
                        if nflat < len(tiles):
                            p1_front(nflat)
                        pa = bank()
                        pb = bank()
                        zmm(pa[64:96, :], 384, 32)
                        zmm(pb[64:96, :], 1440, 32)
                        rq = rstd_from((rt2, rs2), [cqsq[:, 0, :], cqsq[:, 1, :]], QL, T)
                        for j in range(2):
                            sc.tt(cqn[:, j, :], cq[:, j, :], rq[:], ALU.mult)
                        rk = rstd_from((rt, rs3), [cqsq[:, 2, :]], KVL, T)
                        sc.tt(cqn[:, 2, :], cq[:, 2, :], rk[:], ALU.mult)
                        kr_ = krb[i % 2]
                        sc.tt(t1[64:96, :], pa[64:96, :], rp[64:96, 0, :], ALU.mult)
                        sc.tt(t2[64:96, :], pb[64:96, :], rp[64:96, 1, :], ALU.mult)
                        sc.tt(kr_[64:96, :], t1[64:96, :], t2[64:96, :], ALU.add)
                        sc.dma(krT[s][:, t0:t0 + T], kr_[64:96, :], writes=[("krT%d" % s, 0, 1, i, i + 1)])
                        g = glu[i % 2]

                        def glu_chunk(c):
                            pa_ = bank()
                            pg_ = bank()
                            zmm(pa_[:], 416 + c * 128, 128)
                            zmm(pg_[:], 928 + c * 128, 128)
                            sg = sig[c % 2]
                            sc.act(sg[:], pg_[:], AF.Sigmoid)
                            sc.tt(g[:, c, :], pa_[:], sg[:], ALU.mult)

                        def q_nope(j):
                            pq = bank()
                            for kk in range(2):
                                sc.mm(pq[:], Wuq[:, kk, j * 128:(j + 1) * 128], cqn[:, kk, :],
                                      start=(kk == 0), stop=(kk == 1))
                            sc.act(qn[:, j, :], pq[:], AF.Copy)

                        def q_rope(r):
                            pr = bank()
                            pw = bank()
                            for kk in range(2):
                                sc.mm(pr[:], Wuq[:, kk, 512 + r * 128:512 + (r + 1) * 128], cqn[:, kk, :],
                                      start=(kk == 0), stop=(kk == 1))
                            for kk in range(2):
                                sc.mm(pw[:], Wuq[:, kk, 768 + r * 128:768 + (r + 1) * 128], cqn[:, kk, :],
                                      start=(kk == 0), stop=(kk == 1))
                            ta = tq[(2 * r) % 4]
                            tb = tq[(2 * r + 1) % 4]
                            sc.tt(ta[:], pr[:], rp[:, 0, :], ALU.mult)
                            sc.tt(tb[:], pw[:], rp[:, 1, :], ALU.mult)
                            sc.tt(qr[:, r, :], ta[:], tb[:], ALU.add)

                        glu_chunk(0)
                        glu_chunk(1)
                        q_nope(0)
                        q_rope(0)
                        glu_chunk(2)
                        q_nope(1)
                        q_rope(1)
                        glu_chunk(3)
                        q_nope(2)
                        q_nope(3)
                        for b_ in range(2):
                            sc.dma(qT[s].rearrange("(a b) r s -> b r a s", b=2)[b_][0:64, :, t0:t0 + T],
                                   qn[b_ * 64:(b_ + 1) * 64, :, :], writes=[("qT%d" % s, 0, 1, 8 * i + b_, 8 * i + b_ + 1)])
                        for m_ in range(4):
                            sc.dma(qT[s].rearrange("(r m) q s -> m q r s", m=4)[m_][64:96, :, t0:t0 + T],
                                   qr[m_ * 32:(m_ + 1) * 32, :, :],
                                   writes=[("qT%d" % s, 0, 1, 8 * i + 2 + m_, 8 * i + 3 + m_)])
                        sc.dma(gluS[s].rearrange("(c p) s -> p c s", p=128)[:, :, PAD + t0:PAD + t0 + T], g[:],
                               writes=[("gluS%d" % s, 0, 1, i, i + 1)])
                        if nflat < len(tiles):
                            p1_front2(nflat)
                        ko = knb[0]
                        for j in range(4):
                            ps = bank()
                            sc.mm(ps[:], Wkv[:, 0, j * 128:(j + 1) * 128], cqn[:, 2, :])
                            sc.act(ko[:, j, :], ps[:], AF.Copy)
                        sc.dma(knT[s].rearrange("(a b) r s -> (b r) a s", b=2)[:, :, t0:t0 + T], ko[:],
                               writes=[("knT%d" % s, 0, 1, i, i + 1)])
                        vo = vb[0]
                        for k in range(4):
                            ps = bank()
                            sc.mm(ps[:], cqn[:, 2, k * 128:(k + 1) * 128], Wkv[:, 0, 512:1024])
                            sc.act(vo[:, k, :, 0:64], ps[:].rearrange("p (h d) -> p h d", h=NH), AF.Copy)
                        sc.dma(vS[s][t0:t0 + T, :].rearrange("(k p) d -> p k d", p=128),
                               vo[:].rearrange("p k h d -> p k (h d)"),
                               writes=[("vS%d" % s, 0, 1, i, i + 1)])
                sc.barrier()
                sc.emit()

            with ExitStack() as es:
                SM = max(seqs)
                Kb = [sb(es, "Kb%d" % i, [128, SM], BF16) for i in range(2)]
                Vall = sb(es, "Vall", [128, SM // 128, NH * 65], BF16)
                Qb = [sb(es, "Qb%d" % i, [128, T], BF16) for i in range(3)]
                Pt = [sb(es, "Pt%d" % i, [128, 2 * T], BF16) for i in range(3)]
                oaug = [sb(es, "oaug%d" % i, [128, T], F32) for i in range(2)]
                rden = [sb(es, "rden%d" % i, [128, T], F32) for i in range(2)]
                obf = [sb(es, "obf%d" % i, [128, T], BF16) for i in range(2)]
                sel = sb(es, "sel", [128, 64], F32)
                sc.memset(sel[:], 0.0)
                sc.memset(sel[64:65, :], 1.0)
                cwt = sb(es, "cwt", [128, 4 * CW], F32)
                gin = [sb(es, "gin%d" % i, [128, 4, T + 2 * PAD], BF16) for i in range(2)]
                ysb = sb(es, "ysb", [128, 4, T], F32)
                ybf = sb(es, "ybf", [128, 4, T], BF16)
                ysq = sb(es, "ysq", [128, 4, T], BF16)
                mean = sb(es, "mean", [128, T], F32)
                msq = sb(es, "msq", [128, T], F32)
                var = sb(es, "var", [128, T], F32)
                rt3 = sb(es, "rt3", [128, T], F32)
                rs3 = sb(es, "rs3", [128, T], F32)
                tn = [sb(es, "tn%d" % i, [128, T], F32) for i in range(4)]
                sbf = [sb(es, "sbf%d" % i, [128, 4, T], BF16) for i in range(2)]
                sc.dma(cwt[:], cw_in[l])
                cstate = {"bank": None}

                def conv_gen():
                    ctiles = [(s_, j_) for s_ in range(NS) for j_ in range(seqs[s_] // T)]

                    def cload(n):
                        s_, j_ = ctiles[n]
                        sc.dma(gin[n % 2][:],
                               gluS[s_].rearrange("(c p) s -> p c s", p=128)[:, :, j_ * T:(j_ + 1) * T + 2 * PAD])

                    cload(0)
                    for n, (s_, j_) in enumerate(ctiles):
                        if n + 1 < len(ctiles):
                            cload(n + 1)
                        g = gin[n % 2]
                        for cp in range(2):
                            for tap in range(CW):
                                for c in (2 * cp, 2 * cp + 1):
                                    w = cwt[:, c * CW + tap:c * CW + tap + 1]
                                    if tap == 0:
                                        sc.ts(ysb[:, c, :], g[:, c, 0:T], w, cc("conv_b", l, c), ALU.mult, ALU.add)
                                    else:
                                        sc.stt(ysb[:, c, :], g[:, c, tap:tap + T], w, ysb[:, c, :], ALU.mult, ALU.add)
                                    yield
                            for c in (2 * cp, 2 * cp + 1):
                                sc.tt(ysq[:, c, :], ysb[:, c, :], ysb[:, c, :], ALU.mult)
                                yield
                                sc.copy(ybf[:, c, :], ysb[:, c, :])
                                yield
                        while cstate["bank"] is None:
                            yield
                        bk = cstate["bank"]
                        for c in range(4):
                            sc.mm(bk[:], ones_bf[:], ybf[:, c, :], start=(c == 0), stop=(c == 3))
                        sc.ts(mean[:], bk[:], 1.0 / CCH, None, ALU.mult)
                        for c in range(4):
                            sc.mm(bk[:], ones_bf[:], ysq[:, c, :], start=(c == 0), stop=(c == 3))
                        sc.tt(msq[:], mean[:], mean[:], ALU.mult)
                        sc.stt(var[:], bk[:], 1.0 / CCH, msq[:], ALU.mult, ALU.subtract)
                        sc.ts(var[:], var[:], 0.0, None, ALU.max)
                        for _ in range(8):
                            yield
                        sc.act(rt3[:], var[:], AF.Sqrt, bias=epsc[:, 0:1])
                        for _ in range(4):
                            yield
                        sc.recip(rs3[:], rt3[:])
                        yield
                        so = sbf[n % 2]
                        for c in range(4):
                            a_ = tn[c]
                            sc.tt(a_[:], ysb[:, c, :], mean[:], ALU.subtract)
                            yield
                            sc.tt(a_[:], a_[:], rs3[:], ALU.mult)
                            yield
                        for _ in range(8):
                            yield
                        for c in range(4):
                            sc.act(so[:, c, :], tn[c][:], AF.Silu, bias=cc("ln_b", l, c), scale=cc("ln_g", l, c))
                        sc.dma(sT[s_].rearrange("(c p) s -> p c s", p=128)[:, :, j_ * T:(j_ + 1) * T], so[:],
                               writes=[("sT%d" % s_, 0, 1, j_, j_ + 1)])
                        yield

                sgroups = [BV(0, 1024), BV(1024, 1024), BV(2048, 1024)]
                obanks = banks[6:8]
                dbanks = banks[6:8]
                nsc = 0
                iters = [(s, h, qi) for s in range(NS) for h in range(NH) for qi in range(seqs[s] // T)]

                def p2_loads(n):
                    s, h, qi = iters[n]
                    S = seqs[s]
                    NK = S // 128
                    if qi == 0:
                        if h < 2:
                            sc.dma(Kb[h % 2][64:96, 0:S], krT[s][:, :])
                        sc.dma(Kb[h % 2][0:64, 0:S], knT[s][h])
                    sc.dma(Qb[n % 3][0:96, :], qT[s][h][:, qi * T:(qi + 1) * T])

                pending = []

                def fin_b(it, s, h, qi):
                    oa = oaug[it % 2]
                    db = dbanks[it % 2]
                    sc.mm(db[0:64, :], sel[0:65, :], oa[0:65, :])
                    rd = rden[it % 2]
                    sc.recip(rd[0:64, :], db[0:64, :])
                    o_ = obf[it % 2]
                    sc.tt(o_[0:64, :], oa[0:64, :], rd[0:64, :], ALU.mult)
                    sc.dma(oT[s][h * 64:(h + 1) * 64, qi * T:(qi + 1) * T], o_[0:64, :],
                           writes=[("oT%d" % s, 0, 1, it, it + 1)])

                cg = conv_gen()

                def conv_pull(k=1):
                    for _ in range(k):
                        try:
                            next(cg)
                        except StopIteration:
                            break

                p2_loads(0)
                for it, (s, h, qi) in enumerate(iters):
                    S = seqs[s]
                    NK = S // 128
                    if h == 0 and qi == 0:
                        for k0 in range(0, NK, 8):
                            k1 = min(NK, k0 + 8)
                            sc.dma(Vall[:, k0:k1, :],
                                   vS[s][k0 * 128:k1 * 128, :].rearrange("(k p) d -> p k d", p=128))
                    if it + 1 < len(iters):
                        p2_loads(it + 1)
                    K_ = Kb[h % 2]
                    Q_ = Qb[it % 3]
                    ob = obanks[it % 2]
                    scs = {}
                    NG = NK // 2

                    def qk(g):
                        nonlocal nsc
                        grp = sgroups[nsc % 3]
                        nsc += 1
                        scs[g] = grp
                        for j in range(2):
                            kt = 2 * g + j
                            sc.mm(grp[:, j * T:(j + 1) * T], K_[0:96, kt * 128:(kt + 1) * 128], Q_[0:96, :])

                    qk(0)
                    if NG > 1:
                        qk(1)
                    for g in range(NG):
                        if g + 2 < NG:
                            qk(g + 2)
                        if g == 1 and pending:
                            fin_b(*pending.pop())
                        pt = Pt[g % 3]
                        sc.act(pt[:], scs.pop(g)[:], AF.Exp)
                        for j in range(2):
                            kt = 2 * g + j
                            sc.mm(ob[0:65, :], Vall[:, kt, h * 65:(h + 1) * 65], pt[:, j * T:(j + 1) * T],
                                  start=(kt == 0), stop=(kt == NK - 1))
                        cstate["bank"] = obanks[(it + 1) % 2] if (g >= 2 and not pending) else None
                        conv_pull(1)
                    cstate["bank"] = None
                    if pending:
                        fin_b(*pending.pop())
                    sc.copy(oaug[it % 2][0:65, :], ob[0:65, :])
                    pending.append((it, s, h, qi))
                if pending:
                    fin_b(*pending.pop())
                cstate["bank"] = obanks[len(iters) % 2]
                for _ in cg:
                    pass
                sc.barrier()
                sc.emit()

            with ExitStack() as es:
                Wg = sb(es, "Wg", [128, 8, 2048], BF16)
                Woa = sb(es, "Woa", [128, 4, D], BF16)
                Wpc = sb(es, "Wpc", [128, 4, D], BF16)
                Wo = sb(es, "Wo", [128, 8, D], BF16)
                xs = [sb(es, "xs%d" % i, [128, 8, T], F32) for i in range(2)]
                ss_ = [sb(es, "ss%d" % i, [128, 4, T], BF16) for i in range(2)]
                os_ = [sb(es, "os%d" % i, [128, 4, T], BF16) for i in range(2)]
                sq = sb(es, "sq", [128, 8, T], BF16)
                us = [sb(es, "u%d" % i, [128, 8, T], BF16) for i in range(2)]
                rt = sb(es, "rt", [128, T], F32)
                rs = sb(es, "rs", [128, T], F32)
                gA = [sb(es, "gA%d" % i, [128, T], F32) for i in range(2)]
                gC = [sb(es, "gC%d" % i, [128, T], F32) for i in range(2)]
                m1 = [sb(es, "m1%d" % i, [128, T], F32) for i in range(2)]
                m2 = [sb(es, "m2%d" % i, [128, T], F32) for i in range(2)]
                mg = sb(es, "mg", [128, 8, T], BF16)
                load_w(Wg, w_g[l], D, 2048, "g_mix", l)
                load_w(Woa, w_oa[l], 512, D)
                load_w(Wpc, w_pc[l], 512, D)
                load_w(Wo, w_out[l], D, D)
                tiles = [(s_, i_) for s_ in range(NS) for i_ in range(seqs[s_] // T)]

                def p3_loads(n):
                    s_, i_ = tiles[n]
                    sc.dma(xs[n % 2][:], xview(s_, i_ * T, T), reads=[("xT%d" % s_, 0, 1, i_, i_ + 1)])
                    sc.dma(ss_[n % 2][:], sT[s_].rearrange("(c p) s -> p c s", p=128)[:, :, i_ * T:(i_ + 1) * T])
                    sc.dma(os_[n % 2][:], oT[s_].rearrange("(c p) s -> p c s", p=128)[:, :, i_ * T:(i_ + 1) * T])

                def p3_front(n):
                    x = xs[n % 2]
                    sc.act(sq[:], x[:], AF.Square)
                    r = rstd_from((rt, rs), [sq[:, c, :] for c in range(8)], D, T)
                    scale_chunks(us[n % 2], x, r, T)

                p3_loads(0)
                p3_front(0)
                for n, (s, i) in enumerate(tiles):
                    if True:
                        t0 = i * T
                        x = xs[n % 2]
                        s_in = ss_[n % 2]
                        o_in = os_[n % 2]
                        u = us[n % 2]
                        if n + 1 < len(tiles):
                            p3_loads(n + 1)
                        for c in range(8):
                            if c == 5 and n + 1 < len(tiles):
                                p3_front(n + 1)
                            pA = bank()
                            pC = bank()
                            pa = bank()
                            pc_ = bank()
                            for k in range(8):
                                sc.mm(pA[:], Wg[:, k, c * 128:(c + 1) * 128], u[:, k, :], start=(k == 0), stop=(k == 7))
                            for k in range(8):
                                sc.mm(pC[:], Wg[:, k, D + c * 128:D + (c + 1) * 128], u[:, k, :],
                                      start=(k == 0), stop=(k == 7))
                            for k in range(4):
                                sc.mm(pa[:], Woa[:, k, c * 128:(c + 1) * 128], o_in[:, k, :], start=(k == 0), stop=(k == 3))
                            for k in range(4):
                                sc.mm(pc_[:], Wpc[:, k, c * 128:(c + 1) * 128], s_in[:, k, :], start=(k == 0), stop=(k == 3))
                            a = gA[c % 2]
                            b = gC[c % 2]
                            sc.act(a[:], pA[:], AF.Sigmoid, bias=cc("b_gate", l, c))
                            sc.act(b[:], pC[:], AF.Sigmoid, bias=cc("b_gate", l, 8 + c))
                            sc.tt(m1[c % 2][:], pa[:], a[:], ALU.mult)
                            sc.tt(m2[c % 2][:], pc_[:], b[:], ALU.mult)
                            sc.tt(mg[:, c, :], m1[c % 2][:], m2[c % 2][:], ALU.add)
                        for c in range(8):
                            ps = bank()
                            for k in range(8):
                                sc.mm(ps[:], Wo[:, k, c * 128:(c + 1) * 128], mg[:, k, :], start=(k == 0), stop=(k == 7))
                            sc.tt(x[:, c, :], ps[:], x[:, c, :], ALU.add)
                        sc.dma(xview(s, t0, T), x[:], writes=[("xT%d" % s, 0, 1, i, i + 1)])
                sc.barrier()
                sc.emit()

            for half in range(2):
                with ExitStack() as es:
                    T4 = 512
                    FH = 16
                    Wu = sb(es, "Wu", [128, 8, FH * 128], BF16)
                    Wd = sb(es, "Wd", [128, FH, D], BF16)
                    xs = [sb(es, "xs%d" % i, [128, 8, T4], F32) for i in range(2)]
                    sq = sb(es, "sq", [128, 8, T4], BF16) if half == 0 else None
                    xns = [sb(es, "xn%d" % i, [128, 8, T4], BF16) for i in range(2)]
                    rt = sb(es, "rt", [128, T4], F32)
                    rs = sb(es, "rs", [128, T4], F32)
                    rl = [sb(es, "rl%d" % i, [128, T4], F32) for i in range(3)]
                    hbs = [sb(es, "hb%d" % i, [128, FH, T4], BF16) for i in range(2)]
                    load_w(Wu, w_up[l][:, half * FH * 128:(half + 1) * FH * 128], D, FH * 128, "g_mlp", l)
                    load_w(Wd, w_down[l][half * FH * 128:(half + 1) * FH * 128, :], FH * 128, D)
                    tiles = [(s_, i_) for s_ in range(NS) for i_ in range(seqs[s_] // T4)]

                    def xnview(s_, i_):
                        return xnS[s_].rearrange("(c p) s -> p c s", p=128)[:, :, i_ * T4:(i_ + 1) * T4]

                    def p4_loads(n):
                        s_, i_ = tiles[n]
                        sc.dma(xs[n % 2][:], xview(s_, i_ * T4, T4), reads=[("xT%d" % s_, 0, 1, i_, i_ + 1)])
                        if half == 1:
                            sc.dma(xns[n % 2][:], xnview(s_, i_))

                    def p4_front(n):
                        s_, i_ = tiles[n]
                        x = xs[n % 2]
                        sc.act(sq[:], x[:], AF.Square)
                        r = rstd_from((rt, rs), [sq[:, c, :] for c in range(8)], D, T4)
                        scale_chunks(xns[n % 2], x, r, T4)
                        sc.dma(xnview(s_, i_), xns[n % 2][:], writes=[("xnS%d" % s_, 0, 1, i_, i_ + 1)])

                    p4_loads(0)
                    if half == 0:
                        p4_front(0)
                    for n, (s, i) in enumerate(tiles):
                        t0 = i * T4
                        x = xs[n % 2]
                        xn = xns[n % 2]
                        hb = hbs[n % 2]
                        if n + 1 < len(tiles):
                            p4_loads(n + 1)
                        for f in range(FH):
                            if half == 0 and f == 10 and n + 1 < len(tiles):
                                p4_front(n + 1)
                            ps = bank()
                            for k in range(8):
                                sc.mm(ps[:], Wu[:, k, f * 128:(f + 1) * 128], xn[:, k, :],
                                      start=(k == 0), stop=(k == 7))
                            rr = rl[f % 3]
                            sc.act(rr[:], ps[:], AF.Relu)
                            sc.tt(hb[:, f, :], rr[:], rr[:], ALU.mult)
                        for c in range(8):
                            ps = bank()
                            for f in range(FH):
                                sc.mm(ps[:], Wd[:, f, c * 128:(c + 1) * 128], hb[:, f, :],
                                      start=(f == 0), stop=(f == FH - 1))
                            sc.tt(x[:, c, :], ps[:], x[:, c, :], ALU.add)
                        sc.dma(xview(s, t0, T4), x[:], writes=[("xT%d" % s, 0, 1, i, i + 1)])
                    sc.barrier()
                    sc.emit()

            with ExitStack() as es:
                last = (l == DEPTH - 1)
                Wpg = sb(es, "Wpg", [128, 8, D], BF16)
                Wpl = sb(es, "Wpl", [128, 2, D], BF16)
                xs = [sb(es, "xs%d" % i, [128, 8, T], F32) for i in range(2)]
                ptok = [sb(es, "ptok%d" % i, [128, 4, PLE], F32) for i in range(2)]
                pTs = [sb(es, "pT%d" % i, [128, 2, T], BF16) for i in range(2)]
                xbs = [sb(es, "xb%d" % i, [128, 8, T], BF16) for i in range(2)]
                E = sb(es, "E", [128, 8, T], F32)
                esq = sb(es, "esq", [128, 8, T], BF16)
                rt = sb(es, "rt", [128, T], F32)
                rs = sb(es, "rs", [128, T], F32)
                rt2 = sb(es, "rt2", [128, T], F32)
                rs2 = sb(es, "rs2", [128, T], F32)
                sg = [sb(es, "sg%d" % i, [128, T], F32) for i in range(2)]
                en = [sb(es, "en%d" % i, [128, T], F32) for i in range(2)]
                yt = sb(es, "yt", [128, 8, T], F32) if last else None
                fsq = sb(es, "fsq", [128, 8, T], BF16) if last else None
                ytok = [sb(es, "ytok%d" % i, [128, D], F32) for i in range(2)] if last else None
                load_w(Wpg, w_pg[l], D, D)
                load_w(Wpl, w_ple[l], PLE, D)
                tiles = [(s_, i_) for s_ in range(NS) for i_ in range(seqs[s_] // T)]

                def p5_loads(n):
                    s_, i_ = tiles[n]
                    sc.dma(xs[n % 2][:], xview(s_, i_ * T, T), reads=[("xT%d" % s_, 0, 1, i_, i_ + 1)])
                    sc.dma(ptok[n % 2][:], p_in[s_][l][i_ * T:(i_ + 1) * T, :].rearrange("(s p) d -> p s d", p=128))

                def p5_front(n):
                    x = xs[n % 2]
                    pk = ptok[n % 2]
                    for j in range(2):
                        ps = bank()
                        for k in range(4):
                            sc.tr(ps[:, k * 128:(k + 1) * 128], pk[:, k, j * 128:(j + 1) * 128], ident[:])
                        sc.copy(pTs[n % 2][:, j, :], ps[:])
                    sc.act(xbs[n % 2][:, 0:4, :], x[:, 0:4, :], AF.Copy)
                    sc.copy(xbs[n % 2][:, 4:8, :], x[:, 4:8, :])

                fin_q = []
                fin_r = []
                ycnt = 0

                def final_a2():
                    n_, s_, i_ = fin_q[0]
                    x_ = xs[n_ % 2]
                    rf = rstd_from((rt2, rs2), [fsq[:, c, :] for c in range(8)], D, T)
                    for c in range(8):
                        sc.stt(yt[:, c, :], x_[:, c, :], cc("g_final", 0, c), rf[:], ALU.mult, ALU.mult)

                def final_b():
                    nonlocal ycnt
                    n_, s_, i_ = fin_q.pop(0)
                    for k in range(4):
                        yk = ytok[k % 2]
                        for half in range(2):
                            ps = bank()
                            for c4 in range(4):
                                c = half * 4 + c4
                                sc.tr(ps[:, c4 * 128:(c4 + 1) * 128], yt[:, c, k * 128:(k + 1) * 128], ident[:])
                            if half == 0:
                                sc.copy(yk[:, 0:512], ps[:])
                            else:
                                sc.act(yk[:, 512:1024], ps[:], AF.Copy)
                        ycnt += 1
                        sc.dma(y_out[s_][i_ * T + k * 128:i_ * T + (k + 1) * 128, :], yk[:],
                               writes=[("y%d" % s_, 0, 1, ycnt, ycnt + 1)])

                p5_loads(0)
                p5_front(0)
                for n, (s, i) in enumerate(tiles):
                    if True:
                        t0 = i * T
                        x = xs[n % 2]
                        pk = ptok[n % 2]
                        pT = pTs[n % 2]
                        xb = xbs[n % 2]
                        for c in range(8):
                            ps = bank()
                            for j in range(2):
                                sc.mm(ps[:], Wpl[:, j, c * 128:(c + 1) * 128], pT[:, j, :], start=(j == 0), stop=(j == 1))
                            sc.act(E[:, c, :], ps[:], AF.Copy)
                            sc.act(esq[:, c, :], ps[:], AF.Square)
                        if last and fin_q:
                            final_a2()
                        if n + 1 < len(tiles):
                            p5_loads(n + 1)
                        r = rstd_from((rt, rs), [esq[:, c, :] for c in range(8)], D, T)
                        for c in range(8):
                            if c == 4 and n + 1 < len(tiles):
                                p5_front(n + 1)
                            if c == 2 and last and fin_q:
                                final_b()
                            ps = bank()
                            for k in range(8):
                                sc.mm(ps[:], Wpg[:, k, c * 128:(c + 1) * 128], xb[:, k, :], start=(k == 0), stop=(k == 7))
                            g_ = sg[c % 2]
                            e_ = en[c % 2]
                            sc.act(g_[:], ps[:], AF.Sigmoid)
                            sc.stt(e_[:], E[:, c, :], cc("g_ple", l, c), r[:], ALU.mult, ALU.mult)
                            sc.tt(e_[:], e_[:], g_[:], ALU.mult)
                            sc.tt(x[:, c, :], x[:, c, :], e_[:], ALU.add)
                        if not last:
                            sc.dma(xview(s, t0, T), x[:], writes=[("xT%d" % s, 0, 1, i, i + 1)])
                        else:
                            sc.act(fsq[:], x[:], AF.Square)
                            fin_q.append((n, s, i))
                if last:
                    while fin_q:
                        final_a2()
                        final_b()
                sc.barrier()
                sc.emit()
    return nc


def _rope_tables(smax):
    inv = (1.0 / (np.float32(10000.0) ** (np.arange(0, ROPE, 2, dtype=np.float32) / np.float32(ROPE)))).astype(np.float32)
    ang = (np.arange(smax, dtype=np.float32)[:, None] * inv[None, :]).astype(np.float32)
    cos = np.cos(ang).astype(np.float32).T
    sin = np.sin(ang).astype(np.float32).T
    tab = np.zeros((32, 2, smax), np.float32)
    tab[0:16, 0] = cos
    tab[16:32, 0] = cos
    tab[0:16, 1] = -sin
    tab[16:32, 1] = sin
    return np.ascontiguousarray(np.tile(tab, (4, 1, 1)))


def _prep_shared(inp, smax):
    f = lambda a: np.ascontiguousarray(np.asarray(a, dtype=np.float32))
    w_in = f(inp["w_in"])
    krsw = np.concatenate([w_in[:, :, 400:416], w_in[:, :, 384:400]], axis=2)
    w_p1 = np.concatenate([w_in[:, :, 0:1440], krsw], axis=2)
    w_g = w_in[:, :, 1440:3488]
    wuq = f(inp["w_uq"])
    nop, rop, sw = [], [], []
    for h in range(NH):
        b = h * 96 + 64
        nop.append(wuq[:, :, h * 96:h * 96 + 64])
        rop.append(wuq[:, :, b:b + 32])
        sw.append(wuq[:, :, b + 16:b + 32])
        sw.append(wuq[:, :, b:b + 16])
    w_uq = np.concatenate(nop + rop + sw, axis=2)
    wkv = f(inp["w_ukv"]).reshape(DEPTH, KVL, NH, 128)
    w_kv = np.concatenate([wkv[:, :, :, 0:64].reshape(DEPTH, KVL, 512),
                           wkv[:, :, :, 64:128].reshape(DEPTH, KVL, 512)], axis=2)
    cwv = f(inp["conv_w"])
    cw = cwv.reshape(DEPTH, CW, 4, 128).transpose(0, 3, 2, 1).reshape(DEPTH, 128, 4 * CW)
    cst = np.zeros((128, NCST), np.float32)
    for (name, l), c0 in CST.items():
        v = f(inp[name])
        v = v if name == "g_final" else v[l]
        k = v.shape[0] // 128
        cst[:, c0:c0 + k] = v.reshape(k, 128).T
    return {
        "w_p1": f(w_p1), "w_g": f(w_g), "w_uq": f(w_uq), "w_kv": f(w_kv),
        "w_oa": f(inp["w_oa"]), "w_pc": f(inp["w_pc"]), "w_out": f(inp["w_out"]),
        "w_up": f(inp["w_up"]), "w_down": f(inp["w_down"]), "w_pg": f(inp["w_ple_gate"]),
        "w_ple": f(inp["w_ple"]), "cw": f(cw), "cst": cst,
        "ident": np.eye(128, dtype=np.float32), "rope": _rope_tables(smax),
    }


def run_cores(inp, n_cores=8):
    xp = np.asarray(inp["x_prompt"], np.float32)
    xsm = np.asarray(inp["x_sample"], np.float32)
    pp = np.asarray(inp["p_prompt"], np.float32)
    psm = np.asarray(inp["p_sample"], np.float32)
    S0, S1 = xp.shape[1], xsm.shape[1]
    nb0, nb1 = xp.shape[0], xsm.shape[0]
    nc = build_program([S0, S1])
    shared = _prep_shared(inp, max(S0, S1))
    in_maps = []
    for c in range(n_cores):
        m = dict(shared)
        b0 = c % nb0
        b1 = c % nb1
        m["x0"] = np.ascontiguousarray(xp[b0])
        m["x1"] = np.ascontiguousarray(xsm[b1])
        m["p0"] = np.ascontiguousarray(pp[:, b0])
        m["p1"] = np.ascontiguousarray(psm[:, b1])
        in_maps.append(m)
    res = run_bass_kernel_spmd(nc, in_maps, core_ids=list(range(n_cores)))
    yp = np.stack([np.asarray(res.results[b % n_cores]["y0"], np.float32) for b in range(nb0)], axis=0)
    ys = np.stack([np.asarray(res.results[b % n_cores]["y1"], np.float32) for b in range(nb1)], axis=0)
    return yp, ys


def kernel(**inputs):
    yp, ys = run_cores(inputs, 8)
    return (yp, ys)
```
